# Optimizing a Trainium2 kernel written in Bass

```python
import jax, jax.numpy as jnp
from jax import lax
import numpy as np

D_MODEL = 2048
BATCH = 2
SEQ = 8192
DEPTH = 4

GRID_W = 64
CTX_LEN = 256
HEAD_DIM = 128
N_HEADS_A = 8
N_KV_A = 2
N_HEADS_B = 8
N_KV_B = 2
WINDOW = 128
Q_BLOCK = 128
ROPE_THETA = 10000.0
MLA_HEADS = 16
MLA_Q_RANK = 512
MLA_KV_RANK = 256
MLA_NOPE = 128
MLA_ROPE = 64
MLA_V = 128
D_FF = 5632
N_MOD = 9
EPS = 1e-6
NEG_INF = -1e30
N_EVEN = (DEPTH + 1) // 2
N_ODD = DEPTH // 2

QA = N_HEADS_A * HEAD_DIM
KA = N_KV_A * HEAD_DIM
QB = N_HEADS_B * HEAD_DIM
KB = N_KV_B * HEAD_DIM
AB_IN = QA + 2 * KA + QB + 2 * KB
AB_SPLITS = [QA, QA + KA, QA + 2 * KA, QA + 2 * KA + QB, QA + 2 * KA + QB + KB]
AB_OUT = (N_HEADS_A + N_HEADS_B) * HEAD_DIM
C_IN = MLA_Q_RANK + MLA_KV_RANK + MLA_ROPE
MLA_Q_OUT = MLA_HEADS * (MLA_NOPE + MLA_ROPE)
MLA_KV_OUT = MLA_HEADS * (MLA_NOPE + MLA_V)
MLA_OUT = MLA_HEADS * MLA_V

kernel_name = "hybrid_flow_backbone_gqa_swa_mla_macaron"


def rmsnorm(x, g):
    xf = x.astype(jnp.float32)
    y = xf * lax.rsqrt(jnp.mean(xf * xf, axis=-1, keepdims=True) + EPS)
    return (y * g.astype(jnp.float32)).astype(x.dtype)


def adaln_in(h, g, mod, j):
    return rmsnorm(h, g) * (1.0 + mod[..., 3 * j + 1, :]) + mod[..., 3 * j, :]


def swiglu(h, w_gu, w_dn):
    g, u = jnp.split(h @ w_gu, 2, axis=-1)
    return (jax.nn.silu(g) * u) @ w_dn


def ffn_sub(h, mod, j, g, w_gu, w_dn):
    return h + 0.5 * mod[..., 3 * j + 2, :] * swiglu(adaln_in(h, g, mod, j), w_gu, w_dn)


def grid_rope_tables(rows, rot_dim):
    r, col = jnp.meshgrid(jnp.arange(rows), jnp.arange(GRID_W), indexing="ij")
    pos = jnp.stack([r.reshape(-1), col.reshape(-1)], axis=-1).astype(jnp.float32)
    nf = rot_dim // 4
    inv = ROPE_THETA ** (-jnp.arange(nf, dtype=jnp.float32) / nf)
    ang = pos[:, :, None] * inv
    return jnp.cos(ang), jnp.sin(ang)


def rope_2d(x, cos, sin):
    b, n, h, d = x.shape
    nf = d // 4
    xr = x.reshape(b, n, h, 2, 2, nf).astype(jnp.float32)
    x1, x2 = xr[..., 0, :], xr[..., 1, :]
    cs, sn = cos[None, :, None], sin[None, :, None]
    out = jnp.stack([x1 * cs - x2 * sn, x2 * cs + x1 * sn], axis=-2)
    return out.reshape(b, n, h, d).astype(x.dtype)


def gqa_small(q, k, v, sink=None):
    b, n, h, d = q.shape
    n_kv = k.shape[2]
    g = h // n_kv
    qg = q.reshape(b, n, n_kv, g, d)
    s = jnp.einsum("bqhgd,bkhd->bhgqk", qg, k).astype(jnp.float32) * (d ** -0.5)
    if sink is not None:
        sk = jnp.broadcast_to(sink.astype(jnp.float32).reshape(n_kv, g)[None, :, :, None, None], s.shape[:-1] + (1,))
        s = jnp.concatenate([s, sk], axis=-1)
    p = jax.nn.softmax(s, axis=-1)
    if sink is not None:
        p = p[..., :-1]
    o = jnp.einsum("bhgqk,bkhd->bqhgd", p.astype(v.dtype), v)
    return o.reshape(b, n, h, d)


def dense_gqa_blocks(q, k, v):
    b, n, h, d = q.shape
    nblk = n // Q_BLOCK
    qb = q.reshape(b, nblk, Q_BLOCK, h, d).swapaxes(0, 1)
    o = lax.map(lambda qblk: gqa_small(qblk, k, v), qb)
    return o.swapaxes(0, 1).reshape(b, n, h, d)


def window_gqa(q, k, v, k_ctx, v_ctx, sink):
    b, n, h, d = q.shape
    n_kv = k.shape[2]
    g = h // n_kv
    nblk = n // Q_BLOCK
    qb = q.reshape(b, nblk, Q_BLOCK, n_kv, g, d)
    pad = ((0, 0), (Q_BLOCK, Q_BLOCK), (0, 0), (0, 0))

    def band(t):
        tp = jnp.pad(t, pad).reshape(b, nblk + 2, Q_BLOCK, n_kv, d)
        return jnp.concatenate([tp[:, :-2], tp[:, 1:-1], tp[:, 2:]], axis=2)

    kb, vb = band(k), band(v)
    qpos = jnp.arange(n).reshape(nblk, Q_BLOCK)
    kpos = jnp.arange(nblk)[:, None] * Q_BLOCK - Q_BLOCK + jnp.arange(3 * Q_BLOCK)[None, :]
    mask = (jnp.abs(qpos[:, :, None] - kpos[:, None, :]) <= WINDOW) & (kpos[:, None, :] >= 0) & (kpos[:, None, :] < n)
    scale = d ** -0.5
    s_loc = jnp.einsum("bnqhgd,bnkhd->bnhgqk", qb, kb).astype(jnp.float32) * scale
    s_loc = jnp.where(mask[None, :, None, None], s_loc, NEG_INF)
    s_ctx = jnp.einsum("bnqhgd,bkhd->bnhgqk", qb, k_ctx).astype(jnp.float32) * scale
    s_sink = jnp.broadcast_to(sink.astype(jnp.float32).reshape(n_kv, g)[None, None, :, :, None, None], s_loc.shape[:-1] + (1,))
    p = jax.nn.softmax(jnp.concatenate([s_loc, s_ctx, s_sink], axis=-1), axis=-1)
    L = 3 * Q_BLOCK
    m = k_ctx.shape[1]
    o = (jnp.einsum("bnhgqk,bnkhd->bnqhgd", p[..., :L].astype(v.dtype), vb)
         + jnp.einsum("bnhgqk,bkhd->bnqhgd", p[..., L:L + m].astype(v.dtype), v_ctx))
    return o.reshape(b, n, h, d)


def ab_heads(hs, w_in, g_q, g_k):
    b, n, _ = hs.shape
    qa, ka, va, qb, kb, vb = jnp.split(hs @ w_in, AB_SPLITS, axis=-1)
    qa = rmsnorm(qa.reshape(b, n, N_HEADS_A, HEAD_DIM), g_q)
    ka = rmsnorm(ka.reshape(b, n, N_KV_A, HEAD_DIM), g_k)
    va = va.reshape(b, n, N_KV_A, HEAD_DIM)
    qb = qb.reshape(b, n, N_HEADS_B, HEAD_DIM)
    kb = kb.reshape(b, n, N_KV_B, HEAD_DIM)
    vb = vb.reshape(b, n, N_KV_B, HEAD_DIM)
    return qa, ka, va, qb, kb, vb


def mixer_ab(hx, hc, w_in, g_q, g_k, sink, w_out, cos, sin, need_ctx):
    qa_c, ka_c, va_c, qb_c, kb_c, vb_c = ab_heads(hc, w_in, g_q, g_k)
    qa, ka, va, qb, kb, vb = ab_heads(hx, w_in, g_q, g_k)
    qa, ka, qb, kb = rope_2d(qa, cos, sin), rope_2d(ka, cos, sin), rope_2d(qb, cos, sin), rope_2d(kb, cos, sin)
    b, n = hx.shape[0], hx.shape[1]
    oa = dense_gqa_blocks(qa, jnp.concatenate([ka_c, ka], axis=1), jnp.concatenate([va_c, va], axis=1))
    ob = window_gqa(qb, kb, vb, kb_c, vb_c, sink)
    out_x = jnp.concatenate([oa.reshape(b, n, QA), ob.reshape(b, n, QB)], axis=-1) @ w_out
    if not need_ctx:
        return out_x, None
    m = hc.shape[1]
    oa_c = gqa_small(qa_c, ka_c, va_c)
    ob_c = gqa_small(qb_c, kb_c, vb_c, sink)
    out_c = jnp.concatenate([oa_c.reshape(b, m, QA), ob_c.reshape(b, m, QB)], axis=-1) @ w_out
    return out_x, out_c


def mla_heads(hs, w_in, g_cq, g_ckv, w_uq, w_ukv):
    b, n, _ = hs.shape
    cq, ckv, k_rope = jnp.split(hs @ w_in, [MLA_Q_RANK, MLA_Q_RANK + MLA_KV_RANK], axis=-1)
    q = (rmsnorm(cq, g_cq) @ w_uq).reshape(b, n, MLA_HEADS, MLA_NOPE + MLA_ROPE)
    kv = (rmsnorm(ckv, g_ckv) @ w_ukv).reshape(b, n, MLA_HEADS, MLA_NOPE + MLA_V)
    q_nope, q_rope = q[..., :MLA_NOPE], q[..., MLA_NOPE:]
    k_nope, v = kv[..., :MLA_NOPE], kv[..., MLA_NOPE:]
    return q_nope, q_rope, k_nope, k_rope, v


def mla_attend(q_nope, q_rope, k_nope, k_rope, v):
    s = (jnp.einsum("bqhd,bkhd->bhqk", q_nope, k_nope)
         + jnp.einsum("bqhd,bkd->bhqk", q_rope, k_rope)).astype(jnp.float32) * ((MLA_NOPE + MLA_ROPE) ** -0.5)
    p = jax.nn.softmax(s, axis=-1)
    return jnp.einsum("bhqk,bkhd->bqhd", p.astype(v.dtype), v)


def mixer_mla(hx, hc, w_in, g_cq, g_ckv, w_uq, w_ukv, w_out, cos, sin, need_ctx):
    qn_c, qr_c, kn_c, kr_c, v_c = mla_heads(hc, w_in, g_cq, g_ckv, w_uq, w_ukv)
    qn, qr, kn, kr, v = mla_heads(hx, w_in, g_cq, g_ckv, w_uq, w_ukv)
    qr = rope_2d(qr, cos, sin)
    kr = rope_2d(kr[:, :, None, :], cos, sin)[:, :, 0, :]
    kn_all = jnp.concatenate([kn_c, kn], axis=1)
    kr_all = jnp.concatenate([kr_c, kr], axis=1)
    v_all = jnp.concatenate([v_c, v], axis=1)
    b, n = hx.shape[0], hx.shape[1]
    nblk = n // Q_BLOCK
    qn_b = qn.reshape(b, nblk, Q_BLOCK, MLA_HEADS, MLA_NOPE).swapaxes(0, 1)
    qr_b = qr.reshape(b, nblk, Q_BLOCK, MLA_HEADS, MLA_ROPE).swapaxes(0, 1)
    o = lax.map(lambda qs: mla_attend(qs[0], qs[1], kn_all, kr_all, v_all), (qn_b, qr_b))
    out_x = o.swapaxes(0, 1).reshape(b, n, MLA_OUT) @ w_out
    if not need_ctx:
        return out_x, None
    o_c = mla_attend(qn_c, qr_c, kn_c, kr_c, v_c)
    out_c = o_c.reshape(b, hc.shape[1], MLA_OUT) @ w_out
    return out_x, out_c


def setup_inputs(seed: int = 0) -> dict:
    key = jax.random.key(seed)
    ks = jax.random.split(key, 24)

    def nrm(k, shape, scale):
        return jax.random.normal(k, shape, jnp.float32) * scale

    return {
        "x": nrm(ks[0], (BATCH, SEQ, D_MODEL), 1.0),
        "c": nrm(ks[1], (BATCH, D_MODEL), 1.0),
        "ctx": nrm(ks[2], (BATCH, CTX_LEN, D_MODEL), 1.0),
        "c_ctx": nrm(ks[3], (D_MODEL,), 1.0),
        "w_mod": nrm(ks[4], (DEPTH, D_MODEL, N_MOD * D_MODEL), D_MODEL ** -0.5),
        "b_mod": nrm(ks[5], (DEPTH, N_MOD * D_MODEL), 0.01),
        "g_norm": 1.0 + nrm(ks[6], (DEPTH, 3, D_MODEL), 0.02),
        "w_gate_up": nrm(ks[7], (DEPTH, 2, D_MODEL, 2 * D_FF), D_MODEL ** -0.5),
        "w_down": nrm(ks[8], (DEPTH, 2, D_FF, D_MODEL), D_FF ** -0.5),
        "w_in_ab": nrm(ks[9], (N_EVEN, D_MODEL, AB_IN), D_MODEL ** -0.5),
        "g_qnorm_a": 1.0 + nrm(ks[10], (N_EVEN, HEAD_DIM), 0.02),
        "g_knorm_a": 1.0 + nrm(ks[11], (N_EVEN, HEAD_DIM), 0.02),
        "sink_b": nrm(ks[12], (N_EVEN, N_HEADS_B), 0.5),
        "w_out_ab": nrm(ks[13], (N_EVEN, AB_OUT, D_MODEL), AB_OUT ** -0.5),
        "w_in_c": nrm(ks[14], (N_ODD, D_MODEL, C_IN), D_MODEL ** -0.5),
        "g_cq": 1.0 + nrm(ks[15], (N_ODD, MLA_Q_RANK), 0.02),
        "g_ckv": 1.0 + nrm(ks[16], (N_ODD, MLA_KV_RANK), 0.02),
        "w_uq": nrm(ks[17], (N_ODD, MLA_Q_RANK, MLA_Q_OUT), MLA_Q_RANK ** -0.5),
        "w_ukv": nrm(ks[18], (N_ODD, MLA_KV_RANK, MLA_KV_OUT), MLA_KV_RANK ** -0.5),
        "w_out_c": nrm(ks[19], (N_ODD, MLA_OUT, D_MODEL), MLA_OUT ** -0.5),
        "g_final": 1.0 + nrm(ks[20], (D_MODEL,), 0.02),
    }


def reference(x, c, ctx, c_ctx, w_mod, b_mod, g_norm, w_gate_up, w_down, w_in_ab, g_qnorm_a, g_knorm_a,
              sink_b, w_out_ab, w_in_c, g_cq, g_ckv, w_uq, w_ukv, w_out_c, g_final):
    n = x.shape[1]
    rows = n // GRID_W
    cos_h, sin_h = grid_rope_tables(rows, HEAD_DIM)
    cos_m, sin_m = grid_rope_tables(rows, MLA_ROPE)
    sc = jax.nn.silu(c)
    scc = jax.nn.silu(c_ctx)
    h = ctx
    for l in range(DEPTH):
        need_ctx = l < DEPTH - 1
        mod_x = (sc @ w_mod[l] + b_mod[l]).reshape(-1, N_MOD, D_MODEL)[:, None]
        mod_c = (scc @ w_mod[l] + b_mod[l]).reshape(N_MOD, D_MODEL)
        x = ffn_sub(x, mod_x, 0, g_norm[l, 0], w_gate_up[l, 0], w_down[l, 0])
        h = ffn_sub(h, mod_c, 0, g_norm[l, 0], w_gate_up[l, 0], w_down[l, 0])
        ax = adaln_in(x, g_norm[l, 1], mod_x, 1)
        ah = adaln_in(h, g_norm[l, 1], mod_c, 1)
        i = l // 2
        if l % 2 == 0:
            ox, oh = mixer_ab(ax, ah, w_in_ab[i], g_qnorm_a[i], g_knorm_a[i], sink_b[i], w_out_ab[i], cos_h, sin_h, need_ctx)
        else:
            ox, oh = mixer_mla(ax, ah, w_in_c[i], g_cq[i], g_ckv[i], w_uq[i], w_ukv[i], w_out_c[i], cos_m, sin_m, need_ctx)
        x = x + mod_x[..., 5, :] * ox
        x = ffn_sub(x, mod_x, 2, g_norm[l, 2], w_gate_up[l, 1], w_down[l, 1])
        if need_ctx:
            h = h + mod_c[..., 5, :] * oh
            h = ffn_sub(h, mod_c, 2, g_norm[l, 2], w_gate_up[l, 1], w_down[l, 1])
    return rmsnorm(x, g_final)
```

```python
import numpy as np
import ml_dtypes
from contextlib import ExitStack
import concourse.bass as bass
import concourse.mybir as mybir
from concourse.bass_utils import run_bass_kernel_spmd

F32 = mybir.dt.float32
BF16 = mybir.dt.bfloat16
AF = mybir.ActivationFunctionType
ALU = mybir.AluOpType
EPS = 1e-6


class Cfg:
    def __init__(self, D=2048, DFF=5632, SEQ=8192, DEPTH=4):
        self.D, self.DFF, self.SEQ, self.DEPTH = D, DFF, SEQ, DEPTH
        self.DC, self.FC = D // 128, DFF // 128
        self.TOK = SEQ // 4
        self.CT = 64
        self.NT = self.TOK + self.CT
        self.NE, self.NO = (DEPTH + 1) // 2, DEPTH // 2
        self.NCH = 9 * self.DC
        self.PC = -(-self.NCH // 4)
        self.NXS = self.TOK // 512
        self.segs = [(0, 64, 'c')] + [(64 + i * 512, 512, 'x') for i in range(self.NXS)]
        ns = len(self.segs)
        self.groups = [list(range(i, min(i + 2, ns))) for i in range(0, ns, 2)]
        self.XCH = self.TOK // 128


class Buf:
    __slots__ = ('w', 'r')

    def __init__(self):
        self.w = {}
        self.r = {}


class Sched:
    ENG = ('pe', 'act', 'dve', 'pool', 'sp')

    def __init__(self, nds=16):
        self.streams = {e: [] for e in self.ENG}
        self.cnt = {e: 0 for e in self.ENG}
        self.waited = {e: {} for e in self.ENG}
        self.dsems = {'sp': [('dsp', i) for i in range(nds)], 'pool': [('dpl', i) for i in range(nds)],
                      'bg': [('dbg', i) for i in range(8)]}
        self.dval = {}
        for lst in self.dsems.values():
            for k in lst:
                self.dval[k] = 0
        self.drr = {'sp': 0, 'pool': 0, 'bg': 0}
        self.ccval = 0
        self.bg = []
        self.nops = 0

    def _waits(self, eng, reads, writes, extra=()):
        need = {}
        for b in reads:
            for k, v in b.w.items():
                if need.get(k, 0) < v:
                    need[k] = v
        for b in writes:
            for k, v in b.w.items():
                if need.get(k, 0) < v:
                    need[k] = v
            for k, v in b.r.items():
                if need.get(k, 0) < v:
                    need[k] = v
        for k, v in extra:
            if need.get(k, 0) < v:
                need[k] = v
        wl = []
        wd = self.waited[eng]
        for k, v in need.items():
            if k == eng and eng == 'pe':
                continue
            if wd.get(k, 0) >= v:
                continue
            wd[k] = v
            wl.append((k, v))
        return wl

    def _mark(self, ev, reads, writes):
        k, v = ev
        for b in reads:
            if b.r.get(k, 0) < v:
                b.r[k] = v
        for b in writes:
            b.w = {k: v}
            b.r = {}

    def op(self, eng, fn, reads=(), writes=(), signal=True):
        wl = self._waits(eng, reads, writes)
        if signal:
            self.cnt[eng] += 1
            ev = (eng, self.cnt[eng])
        else:
            ev = (eng, self.cnt[eng] + 1)
        self.streams[eng].append((wl, fn, eng if signal else None, 1))
        self._mark(ev, reads, writes)
        self.nops += 1

    def dma(self, eng, out, in_, reads=(), writes=(), pool=None):
        pool = pool or eng
        lst = self.dsems[pool]
        k = lst[self.drr[pool]]
        self.drr[pool] = (self.drr[pool] + 1) % len(lst)
        prev = self.dval[k]
        wl = self._waits(eng, reads, writes, extra=((k, prev),) if prev else ())
        self.dval[k] = prev + 16
        self.streams[eng].append((wl, (lambda e, o=out, i=in_: e.dma_start(out=o, in_=i)), k, 16))
        self._mark((k, prev + 16), reads, writes)
        self.nops += 1

    def cc(self, fn, reads=(), writes=()):
        wl = self._waits('pool', reads, writes)
        self.ccval += 1
        self.streams['pool'].append((wl, fn, 'cc', None))
        self._mark(('cc', self.ccval), reads, writes)

    def fence(self, full=False):
        tg = {e: c for e, c in self.cnt.items() if c > 0}
        for k, v in self.dval.items():
            if v > 0 and (full or k[0] != 'dbg'):
                tg[k] = v
        if self.ccval and full:
            tg['cc'] = self.ccval
        for e in self.ENG:
            wl = []
            wd = self.waited[e]
            for k, v in tg.items():
                if k == e:
                    continue
                if wd.get(k, 0) >= v:
                    continue
                wd[k] = v
                wl.append((k, v))
            if wl:
                self.streams[e].append((wl, None, None, 0))

    def pump(self, n=1):
        for _ in range(n):
            if not self.bg:
                return
            self.bg.pop(0)()

    def flush_bg(self):
        while self.bg:
            self.bg.pop(0)()


class Tl:
    def __init__(self, t, off, K, N):
        self.t, self.off, self.K, self.N = t, off, K, N
        self.bufs = {}

    def s(self, k=0, lo=0, hi=None, p=128, p0=0):
        hi = self.N if hi is None else hi
        c = self.off + k * self.N
        return self.t[p0:p0 + p, c + lo:c + hi]

    def b(self, *key):
        bb = self.bufs.get(key)
        if bb is None:
            bb = self.bufs[key] = Buf()
        return bb


class Arena:
    def __init__(self, t, n):
        self.t, self.n, self.off = t, n, 0

    def reset(self):
        self.off = 0

    def alloc(self, K, N):
        tl = Tl(self.t, self.off, K, N)
        self.off += K * N
        assert self.off <= self.n, (self.off, self.n)
        return tl


class Ring:
    def __init__(self, tl, idx=None):
        self.tl = tl
        self.idx = list(range(tl.K)) if idx is None else idx
        self.i = 0

    def next(self):
        k = self.idx[self.i]
        self.i = (self.i + 1) % len(self.idx)
        return k


class DBufs:
    def __init__(self):
        self.d = {}

    def __call__(self, *key):
        b = self.d.get(key)
        if b is None:
            b = self.d[key] = Buf()
        return b


def pack_defs(cfg, l):
    DC, FC = cfg.DC, cfg.FC
    d = {'gu0': (2 * FC, DC * 128), 'gu1': (2 * FC, DC * 128), 'dn0': (2 * DC, FC * 64), 'dn1': (2 * DC, FC * 64),
         'wo': (DC, 16 * 128)}
    if l % 2 == 0:
        d['ab'] = (40, DC * 128)
        d['v'] = (DC, 512)
    else:
        d['cin'] = (8, DC * 128)
        d['uq'] = (32, 512)
        d['ukk'] = (16, 256)
        d['ukv'] = (16, 256)
    for k, (U, F) in d.items():
        assert U % 4 == 0, (k, U)
    return d


def pieces_of(U, F):
    upr = U // 4
    k = max(1, min(upr, (1 << 20) // (128 * F * 2)))
    return [(u0, min(k, upr - u0)) for u0 in range(0, upr, k)]


def build(cfg):
    D, DC, FC, NT, TOK, DEPTH, PC, NCH = cfg.D, cfg.DC, cfg.FC, cfg.NT, cfg.TOK, cfg.DEPTH, cfg.PC, cfg.NCH
    NE, NO = cfg.NE, cfg.NO
    segs = cfg.segs
    nc = bass.Bass("TRN2", target_bir_lowering=False)
    S = Sched()
    es = ExitStack()
    DB = DBufs()

    def ein(name, shape, dt=F32):
        return nc.dram_tensor(name, list(shape), dt, kind="ExternalInput").ap()

    def dint(name, shape, dt=BF16):
        return nc.dram_tensor(name, list(shape), dt).ap()

    xT = ein('xT', [D, NT])
    wm = ein('wm', [DEPTH * PC * 128, DC * 128])
    bm = ein('bm', [128, DEPTH * PC * 3])
    cv = ein('cv', [128, DC * 3])
    sel = ein('sel', [128, 2])
    gn = ein('gn', [128, DEPTH * 3 * DC])
    gf = ein('gf', [128, DC])
    gqk = ein('gqk', [128, NE * 4])
    snk = ein('snk', [128, NE * 8])
    gc = ein('gc', [128, max(NO, 1) * 6])
    cs128 = ein('cs128', [128, 2 * NT])
    cs64 = ein('cs64', [64, 2 * NT])
    msk = ein('msk', [128, 14 * 512], BF16)
    yT = nc.dram_tensor('yT', [D, TOK], F32, kind="ExternalOutput").ap()
    wsh, wbn, wfull = {}, {}, {}
    for l in range(DEPTH):
        for name, (U, Fc) in pack_defs(cfg, l).items():
            key = (name, l)
            wsh[key] = ein(f'w_{name}_{l}', [U // 4 * 128, Fc])
            wbn[key] = [dint(f'b_{name}_{l}_{pi}', [n * 128, Fc]) for pi, (u0, n) in enumerate(pieces_of(U, Fc))]
            wfull[key] = [dint(f'f_{name}_{l}_{pi}', [4 * n * 128, Fc]) for pi, (u0, n) in enumerate(pieces_of(U, Fc))]
    xres = dint('xres', [D, NT], F32)
    modb = dint('modb', [128, DEPTH * PC * 3], F32)
    modg = dint('modg', [4 * 128, DEPTH * PC * 3], F32)
    qTd = dint('qTd', [16 * 128, NT])
    qrd = dint('qrd', [16 * 64, NT])
    oTd = dint('oTd', [16 * 128, NT])
    kTb = [dint(f'kTb{h}', [128, NT]) for h in range(4)]
    kTg = [dint(f'kTg{h}', [4 * 128, NT]) for h in range(4)]
    vtb = [dint(f'vtb{h}', [NT, 128]) for h in range(4)]
    vtg = [dint(f'vtg{h}', [4 * NT, 128]) for h in range(4)]
    lab = [dint('lab0', [128, NT]), dint('lab1', [128, NT]), dint('lab2', [64, NT])]
    lag = [dint('lag0', [4 * 128, NT]), dint('lag1', [4 * 128, NT]), dint('lag2', [4 * 64, NT])]
    GRP4 = [[0, 1, 2, 3], [4, 5, 6, 7]]

    AR_N = 83 * 1024
    AF_N = 7680
    ARt = es.enter_context(nc.sbuf_tensor('AR', [128, AR_N], BF16))
    AFt = es.enter_context(nc.sbuf_tensor('AFa', [128, AF_N], F32))
    CTn = 2 * DEPTH * NCH + 2 * 2 * DEPTH * 3 * DC + DEPTH * 3 * DC + DC + NE * 4 + 2 * NE * 8 + max(NO, 1) * 6 + 2 + 8
    CTt = es.enter_context(nc.sbuf_tensor('CT', [128, CTn], F32))
    ONt = es.enter_context(nc.sbuf_tensor('ON', [128, 128], BF16))
    ONFt = es.enter_context(nc.sbuf_tensor('ONF', [128, 128], F32))
    PSt = es.enter_context(nc.psum_tensor('PS', [128, 8 * 512], F32))
    AR = Arena(ARt, AR_N)
    AFa = Arena(AFt, AF_N)
    CT = Arena(CTt, CTn)
    P = Tl(PSt, 0, 8, 512)
    ones = Tl(ONt, 0, 1, 128)
    onesf = Tl(ONFt, 0, 1, 128)
    modL = CT.alloc(2 * DEPTH, NCH)
    gsT = CT.alloc(2 * DEPTH * 3, DC)
    gateT = CT.alloc(2 * DEPTH * 3, DC)
    gnT = CT.alloc(DEPTH * 3, DC)
    gfT = CT.alloc(1, DC)
    gqkT = CT.alloc(NE, 4)
    snkT = CT.alloc(NE, 8)
    esnkT = CT.alloc(NE, 8)
    gcT = CT.alloc(max(NO, 1), 6)
    selT = CT.alloc(1, 2)
    CONST = Buf()

    def phase():
        S.fence()
        AR.reset()
        AFa.reset()

    def mcol(kind, l, j, dc):
        return modL.s(kind * DEPTH + l, j * DC + dc, j * DC + dc + 1)

    def gscol(kind, l, sub, dc):
        return gsT.s((kind * DEPTH + l) * 3 + sub, dc, dc + 1)

    def gatecol(kind, l, sub, dc):
        return gateT.s((kind * DEPTH + l) * 3 + sub, dc, dc + 1)

    S.op('dve', lambda e: e.memset(ones.s(), 1.0), writes=[CONST])
    S.op('dve', lambda e: e.memset(onesf.s(), 1.0), writes=[CONST])
    for tl, src, n in ((gnT, gn, DEPTH * 3 * DC), (gfT, gf, DC), (gqkT, gqk, NE * 4), (snkT, snk, NE * 8),
                       (gcT, gc, max(NO, 1) * 6), (selT, sel, 2)):
        S.dma('sp', tl.t[:, tl.off:tl.off + n], src[:, :], writes=[CONST])
    for dc in range(DC):
        for si, (o, w, kd) in enumerate(segs):
            S.dma('pool', xres[dc * 128:(dc + 1) * 128, o:o + w], xT[dc * 128:(dc + 1) * 128, o:o + w],
                  writes=[DB('xres', si, dc)])
    S.op('act', lambda e: e.activation(out=esnkT.t[:, esnkT.off:esnkT.off + NE * 8],
                                       in_=snkT.t[:, snkT.off:snkT.off + NE * 8], func=AF.Exp),
         reads=[CONST], writes=[CONST])

    def mod_phase():
        U = DEPTH * PC
        cvt = AFa.alloc(1, DC * 3)
        bmt = AFa.alloc(1, U * 3)
        msh = AFa.alloc(1, U * 3)
        mg = AFa.alloc(1, 4 * U * 3)
        tmpx = AFa.alloc(1, 4 * U)
        sct = AFa.alloc(DC, 3)
        wmr = AFa.alloc(2, DC * 128)
        ring = Ring(wmr)
        S.dma('sp', cvt.s(), cv[:, :], writes=[cvt.b()])
        S.dma('sp', bmt.s(), bm[:, :], writes=[bmt.b()])
        S.op('act', lambda e: e.activation(out=sct.t[:, sct.off:sct.off + DC * 3], in_=cvt.s(), func=AF.Silu),
             reads=[cvt.b()], writes=[sct.b()])
        for u in range(U):
            k = ring.next()
            S.dma('sp', wmr.s(k), wm[u * 128:(u + 1) * 128, :], writes=[wmr.b(k)])
            for kc in range(DC):
                S.op('pe', lambda e, k=k, kc=kc, u=u: e.matmul(
                    P.s(0, u * 3, u * 3 + 3), lhsT=wmr.s(k, kc * 128, kc * 128 + 128), rhs=sct.s(kc),
                    start=(kc == 0), stop=(kc == DC - 1)),
                    reads=[wmr.b(k), sct.b()], writes=[P.b(0)], signal=(kc == DC - 1))
        S.op('dve', lambda e: e.tensor_tensor(out=msh.s(), in0=P.s(0, 0, U * 3), in1=bmt.s(), op=ALU.add),
             reads=[P.b(0), bmt.b()], writes=[msh.b()])
        S.dma('pool', modb[:, :], msh.s(), reads=[msh.b()], writes=[DB('modb')])
        S.cc(lambda e: e.collective_compute("AllGather", ALU.bypass, replica_groups=GRP4,
                                            ins=[modb[:, :]], outs=[modg[:, :]]),
             reads=[DB('modb')], writes=[DB('modg')])
        S.dma('sp', mg.s().rearrange("p (r n) -> p r n", r=4), modg.rearrange("(r p) n -> p r n", p=128),
              reads=[DB('modg')], writes=[mg.b()])
        mg3 = mg.s().rearrange("p (m r) -> p m r", r=3)
        S.op('dve', lambda e: e.tensor_scalar(out=tmpx.s(), in0=mg3[:, :, 0], scalar1=selT.s(0, 0, 1), scalar2=None,
                                              op0=ALU.mult), reads=[mg.b(), CONST], writes=[tmpx.b()])
        S.op('dve', lambda e: e.scalar_tensor_tensor(out=tmpx.s(), in0=mg3[:, :, 1], scalar=selT.s(0, 1, 2),
                                                     in1=tmpx.s(), op0=ALU.mult, op1=ALU.add),
             reads=[mg.b(), tmpx.b(), CONST], writes=[tmpx.b()])
        for r8 in range(4):
            c0 = r8 * PC
            n = min(PC, NCH - c0)
            if n <= 0:
                continue
            for l in range(DEPTH):
                o = (r8 * DEPTH + l) * PC
                S.op('dve', lambda e, l=l, c0=c0, n=n, o=o: e.tensor_copy(
                    out=modL.s(l, c0, c0 + n), in_=tmpx.s(0, o, o + n)), reads=[tmpx.b()], writes=[CONST])
                S.op('dve', lambda e, l=l, c0=c0, n=n, o=o: e.tensor_copy(
                    out=modL.s(DEPTH + l, c0, c0 + n), in_=mg3[:, o:o + n, 2]), reads=[mg.b()], writes=[CONST])
        for kind in range(2):
            for l in range(DEPTH):
                for sub in range(3):
                    r = (kind * DEPTH + l) * 3 + sub
                    ml = kind * DEPTH + l
                    S.op('dve', lambda e, r=r, ml=ml, sub=sub, l=l: e.scalar_tensor_tensor(
                        out=gsT.s(r), in0=modL.s(ml, (3 * sub + 1) * DC, (3 * sub + 2) * DC), scalar=1.0,
                        in1=gnT.s(l * 3 + sub), op0=ALU.add, op1=ALU.mult), reads=[CONST], writes=[CONST])
                    S.op('dve', lambda e, r=r, ml=ml, sub=sub: e.tensor_scalar(
                        out=gateT.s(r), in0=modL.s(ml, (3 * sub + 2) * DC, (3 * sub + 3) * DC),
                        scalar1=(1.0 if sub == 1 else 0.5), scalar2=None, op0=ALU.mult),
                        reads=[CONST], writes=[CONST])

    def queue_weights(l, only=None, skip=()):
        for name, (U, Fc) in pack_defs(cfg, l).items():
            if (only is not None and name not in only) or name in skip:
                continue
            key = (name, l)
            pcs = pieces_of(U, Fc)
            for pi, (u0, n) in enumerate(pcs):
                for j in range(n):
                    S.bg.append(lambda key=key, pi=pi, u0=u0, j=j: S.dma(
                        'pool', wbn[key][pi][j * 128:(j + 1) * 128, :], wsh[key][(u0 + j) * 128:(u0 + j + 1) * 128, :],
                        writes=[DB('wbn', key, pi, j)], pool='bg'))
            for pi, (u0, n) in enumerate(pcs):
                S.bg.append(lambda key=key, pi=pi, n=n: S.cc(
                    lambda e, key=key, pi=pi: e.collective_compute(
                        "AllGather", ALU.bypass, replica_groups=GRP4,
                        ins=[wbn[key][pi][:, :]], outs=[wfull[key][pi][:, :]]),
                    reads=[DB('wbn', key, pi, j) for j in range(n)], writes=[DB('wfull', key, pi)]))

    def wloc(key, u):
        U, Fc = pack_defs(cfg, key[1])[key[0]]
        upr = U // 4
        r, ul = divmod(u, upr)
        pcs = pieces_of(U, Fc)
        k = pcs[0][1]
        pi = ul // k
        n = pcs[pi][1]
        row = (r * n + (ul - pcs[pi][0])) * 128
        return pi, row

    def wunit(key, u):
        pi, row = wloc(key, u)
        return wfull[key][pi][row:row + 128, :]

    def wbuf(key, u):
        return DB('wfull', key, wloc(key, u)[0])

    def norm_seg(l, sub, si, a, aoff, xring, xr, sqr, sq, tmr, tm, rstd, statbank, slot=0):
        for _ in norm_seg_gen(l, sub, si, a, aoff, xring, xr, sqr, sq, tmr, tm, rstd, statbank, slot):
            pass

    def norm_seg_gen(l, sub, si, a, aoff, xring, xr, sqr, sq, tmr, tm, rstd, statbank, slot=0):
        o, w, kd = segs[si]
        kind = 0 if kd == 'x' else 1
        for dc in range(DC):
            k = xring.next()
            S.dma('sp', xr.s(k, 0, w), xres[dc * 128:(dc + 1) * 128, o:o + w],
                  reads=[DB('xres', si, dc)], writes=[xr.b(k)])
            q = sqr.next()
            S.op('act', lambda e, k=k, q=q: e.activation(out=sq.s(q, 0, w), in_=xr.s(k, 0, w), func=AF.Square),
                 reads=[xr.b(k)], writes=[sq.b(q)])
            S.op('pe', lambda e, q=q, dc=dc: e.matmul(P.s(statbank, 0, w), lhsT=ones.s(), rhs=sq.s(q, 0, w),
                                                     start=(dc == 0), stop=(dc == DC - 1)),
                 reads=[sq.b(q), CONST], writes=[P.b(statbank)])
            yield
        S.op('act', lambda e: e.activation(out=rstd.s(0, 0, w), in_=P.s(statbank, 0, w), func=AF.Sqrt,
                                           scale=1.0 / D, bias=EPS), reads=[P.b(statbank)], writes=[rstd.b()])
        S.op('dve', lambda e: e.reciprocal(out=rstd.s(0, 0, w), in_=rstd.s(0, 0, w)),
             reads=[rstd.b()], writes=[rstd.b()])
        for dc in range(DC):
            k = xring.next()
            S.dma('sp', xr.s(k, 0, w), xres[dc * 128:(dc + 1) * 128, o:o + w],
                  reads=[DB('xres', si, dc)], writes=[xr.b(k)])
            t = tmr.next()
            S.op('dve', lambda e, k=k, t=t, dc=dc: e.scalar_tensor_tensor(
                out=tm.s(t, 0, w), in0=xr.s(k, 0, w), scalar=gscol(kind, l, sub, dc), in1=rstd.s(0, 0, w),
                op0=ALU.mult, op1=ALU.mult), reads=[xr.b(k), rstd.b(), CONST], writes=[tm.b(t)])
            S.op('act', lambda e, t=t, dc=dc: e.activation(
                out=a.s(dc, aoff, aoff + w), in_=tm.s(t, 0, w), func=AF.Identity,
                bias=mcol(kind, l, 3 * sub, dc), scale=1.0), reads=[tm.b(t), CONST], writes=[a.b(dc, slot)])
            yield

    def _pr_load_full(key, oc, wr, wk, nk):
        S.dma('sp', wr.s(wk, 0, nk * 128), wunit(key, oc), reads=[wbuf(key, oc)], writes=[wr.b(wk)])

    def _pr_load_halves(key, oc, wr, wk, nk):
        hk = nk // 2
        for hf in range(2):
            S.dma('sp', wr.s(wk, hf * hk * 128, (hf + 1) * hk * 128), wunit(key, 2 * oc + hf),
                  reads=[wbuf(key, 2 * oc + hf)], writes=[wr.b(wk)])

    def proj_residual(l, sub, key, nk, rhs_of, seg_list, psr, xring, xr, xor_, xo, rd_bufs_of, gen=None, gsteps=0):
        wr = proj_residual.wr
        wring = proj_residual.wring
        for oc in range(DC):
            wk = wring.next()
            proj_residual.load(key, oc, wr, wk, nk)
            for si in seg_list:
                o, w, kd = segs[si]
                kind = 0 if kd == 'x' else 1
                pb = psr.next()
                for kk in range(nk):
                    rhs_ap = rhs_of(kk, si)
                    S.op('pe', lambda e, pb=pb, wk=wk, kk=kk, si=si, w=w, rhs_ap=rhs_ap, wr=wr: e.matmul(
                        P.s(pb, 0, w), lhsT=wr.s(wk, kk * 128, kk * 128 + 128), rhs=rhs_ap,
                        start=(kk == 0), stop=(kk == nk - 1)),
                        reads=[wr.b(wk)] + rd_bufs_of(kk, si), writes=[P.b(pb)], signal=(kk == nk - 1))
                k = xring.next()
                S.dma('sp', xr.s(k, 0, w), xres[oc * 128:(oc + 1) * 128, o:o + w],
                      reads=[DB('xres', si, oc)], writes=[xr.b(k)])
                ko = xor_.next()
                S.op('dve', lambda e, pb=pb, k=k, ko=ko, w=w, kind=kind, oc=oc: e.scalar_tensor_tensor(
                    out=xo.s(ko, 0, w), in0=P.s(pb, 0, w), scalar=gatecol(kind, l, sub, oc), in1=xr.s(k, 0, w),
                    op0=ALU.mult, op1=ALU.add), reads=[P.b(pb), xr.b(k), CONST], writes=[xo.b(ko)])
                S.dma('pool', xres[oc * 128:(oc + 1) * 128, o:o + w], xo.s(ko, 0, w),
                      reads=[xo.b(ko)], writes=[DB('xres', si, oc)])
                if gen is not None:
                    for _ in range(gsteps):
                        next(gen, None)
            S.pump(2)
        if gen is not None:
            for _ in gen:
                pass

    def ffn_phase(l, sub, widx, skip_ctx=False):
        phase()
        GW = 1024
        a = AR.alloc(DC, GW)
        h = AR.alloc(FC, GW)
        gur = AR.alloc(4, DC * 128)
        dnr = AR.alloc(2, FC * 128)
        sq = AR.alloc(3, 512)
        xr = AFa.alloc(4, 512)
        xo = AFa.alloc(2, 512)
        tm = AFa.alloc(3, 512)
        rstd = AFa.alloc(1, 512)
        xring, sqr, tmr, xor_ = Ring(xr), Ring(sq), Ring(tm), Ring(xo)
        guring = Ring(gur)
        proj_residual.wr = dnr
        proj_residual.wring = Ring(dnr)
        proj_residual.load = _pr_load_halves
        pgr, pur, pyr = Ring(P, [0, 1]), Ring(P, [2, 3]), Ring(P, [4, 5])
        kgu, kdn = (f'gu{widx}', l), (f'dn{widx}', l)
        glist = []
        for grp in cfg.groups:
            sl = [si for si in grp if not (skip_ctx and segs[si][2] == 'c')]
            if sl:
                glist.append(sl)

        def norm_group_gen(sl):
            for n_, si in enumerate(sl):
                yield from norm_seg_gen(l, sub, si, a, n_ * 512, xring, xr, sqr, sq, tmr, tm, rstd, 6, n_)

        for _ in norm_group_gen(glist[0]):
            pass
        for gi, sl in enumerate(glist):
            offs = {si: n_ * 512 for n_, si in enumerate(sl)}
            slot = {si: n_ for n_, si in enumerate(sl)}
            for fc in range(FC):
                wg, wu = guring.next(), guring.next()
                S.dma('sp', gur.s(wg), wunit(kgu, 2 * fc), reads=[wbuf(kgu, 2 * fc)], writes=[gur.b(wg)])
                S.dma('sp', gur.s(wu), wunit(kgu, 2 * fc + 1), reads=[wbuf(kgu, 2 * fc + 1)], writes=[gur.b(wu)])
                for si in sl:
                    w = segs[si][1]
                    ao = offs[si]
                    pg, pu = pgr.next(), pur.next()
                    for pb, wk in ((pg, wg), (pu, wu)):
                        for dc in range(DC):
                            S.op('pe', lambda e, pb=pb, wk=wk, dc=dc, ao=ao, w=w: e.matmul(
                                P.s(pb, 0, w), lhsT=gur.s(wk, dc * 128, dc * 128 + 128), rhs=a.s(dc, ao, ao + w),
                                start=(dc == 0), stop=(dc == DC - 1)),
                                reads=[gur.b(wk), a.b(dc, slot[si])], writes=[P.b(pb)], signal=(dc == DC - 1))
                    t = tmr.next()
                    S.op('act', lambda e, t=t, pg=pg, w=w: e.activation(out=tm.s(t, 0, w), in_=P.s(pg, 0, w),
                                                                        func=AF.Silu),
                         reads=[P.b(pg)], writes=[tm.b(t)])
                    S.op('dve', lambda e, t=t, pu=pu, w=w, fc=fc, ao=ao: e.tensor_tensor(
                        out=h.s(fc, ao, ao + w), in0=tm.s(t, 0, w), in1=P.s(pu, 0, w), op=ALU.mult),
                        reads=[tm.b(t), P.b(pu)], writes=[h.b(fc, slot[si])])
                S.pump(1)
            gen = norm_group_gen(glist[gi + 1]) if gi + 1 < len(glist) else None
            nsteps = DC * len(sl)
            gst = 0 if gen is None else -(-(len(glist[gi + 1]) * (2 * DC + 1)) // nsteps)
            proj_residual(l, sub, kdn, FC, lambda kk, si: h.s(kk, offs[si], offs[si] + segs[si][1]), sl, pyr,
                          xring, xr, xor_, xo, lambda kk, si: [h.b(kk, slot[si])], gen=gen, gsteps=gst)

    def rope_store(psq, psqs, w, o, cs, csb, gcol, gscol_, rsb, rs, tm, tmr, outb, outr, dst, dstbuf, npart):
        t1, t2 = tmr.next(), tmr.next()
        if gcol is not None:
            S.op('dve', lambda e: e.scalar_tensor_tensor(out=tm.s(t1, 0, w, p=npart), in0=P.s(psq, 0, w, p=npart),
                                                         scalar=gcol, in1=cs.s(0, 0, w, p=npart), op0=ALU.mult,
                                                         op1=ALU.mult),
                 reads=[P.b(psq), csb, CONST], writes=[tm.b(t1)])
            S.op('dve', lambda e: e.scalar_tensor_tensor(out=tm.s(t2, 0, w, p=npart), in0=P.s(psqs, 0, w, p=npart),
                                                         scalar=gscol_, in1=cs.s(1, 0, w, p=npart), op0=ALU.mult,
                                                         op1=ALU.mult),
                 reads=[P.b(psqs), csb, CONST], writes=[tm.b(t2)])
        else:
            S.op('dve', lambda e: e.tensor_tensor(out=tm.s(t1, 0, w, p=npart), in0=P.s(psq, 0, w, p=npart),
                                                  in1=cs.s(0, 0, w, p=npart), op=ALU.mult),
                 reads=[P.b(psq), csb], writes=[tm.b(t1)])
            S.op('dve', lambda e: e.tensor_tensor(out=tm.s(t2, 0, w, p=npart), in0=P.s(psqs, 0, w, p=npart),
                                                  in1=cs.s(1, 0, w, p=npart), op=ALU.mult),
                 reads=[P.b(psqs), csb], writes=[tm.b(t2)])
        ob = outr.next()
        if rs is not None:
            S.op('dve', lambda e: e.tensor_tensor(out=tm.s(t1, 0, w, p=npart), in0=tm.s(t1, 0, w, p=npart),
                                                  in1=tm.s(t2, 0, w, p=npart), op=ALU.add),
                 reads=[tm.b(t1), tm.b(t2)], writes=[tm.b(t1)])
            S.op('dve', lambda e: e.tensor_tensor(out=outb.s(ob, 0, w, p=npart), in0=tm.s(t1, 0, w, p=npart),
                                                  in1=rs.s(0, 0, w, p=npart), op=ALU.mult),
                 reads=[tm.b(t1), rsb], writes=[outb.b(ob)])
        else:
            S.op('dve', lambda e: e.tensor_tensor(out=outb.s(ob, 0, w, p=npart), in0=tm.s(t1, 0, w, p=npart),
                                                  in1=tm.s(t2, 0, w, p=npart), op=ALU.add),
                 reads=[tm.b(t1), tm.b(t2)], writes=[outb.b(ob)])
        S.dma('pool', dst, outb.s(ob, 0, w, p=npart), reads=[outb.b(ob)], writes=[dstbuf])

    def head_rs(psq, w, nfeat, sq, sqr, ssbank, rs):
        q = sqr.next()
        S.op('act', lambda e: e.activation(out=sq.s(q, 0, w), in_=P.s(psq, 0, w), func=AF.Square),
             reads=[P.b(psq)], writes=[sq.b(q)])
        S.op('pe', lambda e: e.matmul(P.s(ssbank, 0, w), lhsT=ones.s(), rhs=sq.s(q, 0, w), start=True, stop=True),
             reads=[sq.b(q), CONST], writes=[P.b(ssbank)])
        S.op('act', lambda e: e.activation(out=rs.s(0, 0, w), in_=P.s(ssbank, 0, w), func=AF.Sqrt,
                                           scale=1.0 / nfeat, bias=EPS), reads=[P.b(ssbank)], writes=[rs.b()])
        S.op('dve', lambda e: e.reciprocal(out=rs.s(0, 0, w), in_=rs.s(0, 0, w)), reads=[rs.b()], writes=[rs.b()])

    def qkv_even(l):
        phase()
        e_ = l // 2
        kab, kv = ('ab', l), ('v', l)
        a = AR.alloc(DC, 512)
        wr = AR.alloc(4, DC * 128)
        wv = AR.alloc(DC, 512)
        sq = AR.alloc(3, 512)
        outb = AR.alloc(3, 512)
        xr = AFa.alloc(3, 512)
        tm = AFa.alloc(4, 512)
        rstd = AFa.alloc(1, 512)
        rs = AFa.alloc(1, 512)
        cs = AFa.alloc(2, 512)
        xring, sqr, tmr, wring, outr = Ring(xr), Ring(sq), Ring(tm), Ring(wr), Ring(outb)
        pqr, pqsr, ssr, pvr = Ring(P, [0, 1]), Ring(P, [2, 3]), Ring(P, [4]), Ring(P, [5, 7])
        S.dma('sp', wv.t[:, wv.off:wv.off + DC * 512].rearrange("p (k n) -> p k n", k=DC),
              wfull[kv][0].rearrange("(k p) n -> p k n", p=128), reads=[DB('wfull', kv, 0)], writes=[wv.b()])
        for si, (o, w, kd) in enumerate(segs):
            norm_seg(l, 1, si, a, 0, xring, xr, sqr, sq, tmr, tm, rstd, 6)
            S.dma('sp', cs.s(0, 0, w), cs128[:, o:o + w], writes=[cs.b()])
            S.dma('sp', cs.s(1, 0, w), cs128[:, NT + o:NT + o + w], writes=[cs.b()])
            jobs = []
            for hh in range(8):
                jobs.append((hh, 8 + hh, 0, qTd, hh * 128, ('qTd', hh, si)))
            for hh in range(2):
                jobs.append((16 + hh, 18 + hh, 2, kTb[hh], 0, ('kTb', hh, si)))
            for hh in range(8):
                jobs.append((20 + hh, 28 + hh, None, qTd, (8 + hh) * 128, ('qTd', 8 + hh, si)))
            for hh in range(2):
                jobs.append((36 + hh, 38 + hh, None, kTb[2 + hh], 0, ('kTb', 2 + hh, si)))
            for (u0, u1, gi, dst, drow, dkey) in jobs:
                pbs = []
                for u, rg in ((u0, pqr), (u1, pqsr)):
                    wk = wring.next()
                    S.dma('sp', wr.s(wk), wunit(kab, u), reads=[wbuf(kab, u)], writes=[wr.b(wk)])
                    pb = rg.next()
                    pbs.append(pb)
                    for dc in range(DC):
                        S.op('pe', lambda e, pb=pb, wk=wk, dc=dc, w=w: e.matmul(
                            P.s(pb, 0, w), lhsT=wr.s(wk, dc * 128, dc * 128 + 128), rhs=a.s(dc, 0, w),
                            start=(dc == 0), stop=(dc == DC - 1)),
                            reads=[wr.b(wk), a.b(dc, 0)], writes=[P.b(pb)], signal=(dc == DC - 1))
                if gi is not None:
                    head_rs(pbs[0], w, 128, sq, sqr, ssr.next(), rs)
                    rope_store(pbs[0], pbs[1], w, o, cs, cs.b(), gqkT.s(e_, gi, gi + 1), gqkT.s(e_, gi + 1, gi + 2),
                               rs.b(), rs, tm, tmr, outb, outr, dst[drow:drow + 128, o:o + w], DB(*dkey), 128)
                else:
                    rope_store(pbs[0], pbs[1], w, o, cs, cs.b(), None, None, None, None, tm, tmr, outb, outr,
                               dst[drow:drow + 128, o:o + w], DB(*dkey), 128)
            for tb in range(0, w, 128):
                tw = min(128, w - tb)
                pb = pvr.next()
                for dc in range(DC):
                    S.op('pe', lambda e, pb=pb, dc=dc, tb=tb, tw=tw: e.matmul(
                        P.s(pb, 0, 512, p=tw), lhsT=a.s(dc, tb, tb + tw), rhs=wv.s(dc),
                        start=(dc == 0), stop=(dc == DC - 1)),
                        reads=[wv.b(), a.b(dc, 0)], writes=[P.b(pb)], signal=(dc == DC - 1))
                ob = outr.next()
                S.op('act', lambda e, pb=pb, ob=ob, tw=tw: e.activation(out=outb.s(ob, 0, 512, p=tw),
                                                                       in_=P.s(pb, 0, 512, p=tw), func=AF.Identity),
                     reads=[P.b(pb)], writes=[outb.b(ob)])
                for hv in range(4):
                    S.dma('pool', vtb[hv][o + tb:o + tb + tw, :], outb.s(ob, hv * 128, hv * 128 + 128, p=tw),
                          reads=[outb.b(ob)], writes=[DB('vtb', hv, si, tb)])
        for hv in range(4):
            S.cc(lambda e, hv=hv: e.collective_compute("AllGather", ALU.bypass, replica_groups=GRP4,
                                                       ins=[kTb[hv][:, :]], outs=[kTg[hv][:, :]]),
                 reads=[DB('kTb', hv, si) for si in range(len(segs))], writes=[DB('kTg', hv)])
            S.cc(lambda e, hv=hv: e.collective_compute("AllGather", ALU.bypass, replica_groups=GRP4,
                                                       ins=[vtb[hv][:, :]], outs=[vtg[hv][:, :]]),
                 reads=[DB('vtb', hv, si, tb) for si, (o, w, kd) in enumerate(segs) for tb in range(0, w, 128)],
                 writes=[DB('vtg', hv)])

    def attend(q_loads, chunks_of, pv_of, den_extra, scale, si, head, oring, ob_t, strr, str_t, pring, p_t,
               rdt, sbank_ring, obank, dbank, acc=None, accr=None):
        o, w, kd = segs[si]
        chunks = chunks_of(si)
        nchk = len(chunks)
        sb = [None] * nchk

        def emit_qk(i):
            kw, qk_ops, _, _, _ = chunks[i]
            pb = sbank_ring.next()
            sb[i] = pb
            n = len(qk_ops)
            for j, (lhsT, rhs, rd) in enumerate(qk_ops):
                S.op('pe', lambda e, pb=pb, lhsT=lhsT, rhs=rhs, j=j, n=n, kw=kw: e.matmul(
                    P.s(pb, 0, w, p=kw), lhsT=lhsT, rhs=rhs, start=(j == 0), stop=(j == n - 1)),
                    reads=rd, writes=[P.b(pb)], signal=(j == n - 1))

        accs = (accr.next(), accr.next(), accr.next())
        aeng = ('dve', 'pool', 'dve')
        for n_, ak in enumerate(accs):
            S.op(aeng[n_], lambda e, ak=ak: e.memset(acc.s(ak, 0, w), 0.0), writes=[acc.b(ak)])
        emit_qk(0)
        if nchk > 1:
            emit_qk(1)
        for i in range(nchk):
            if i + 2 < nchk:
                emit_qk(i + 2)
            kw, _, vl, vrd, mask = chunks[i]
            pb = sb[i]
            pk = pring.next()
            if mask is None:
                S.op('act', lambda e, pb=pb, pk=pk, kw=kw: e.activation(out=p_t.s(pk, 0, w, p=kw),
                                                                      in_=P.s(pb, 0, w, p=kw), func=AF.Exp,
                                                                      scale=scale),
                     reads=[P.b(pb)], writes=[p_t.b(pk)])
            else:
                sk = strr.next()
                S.op('act', lambda e, pb=pb, sk=sk, kw=kw: e.activation(out=str_t.s(sk, 0, w, p=kw),
                                                                      in_=P.s(pb, 0, w, p=kw), func=AF.Exp,
                                                                      scale=scale),
                     reads=[P.b(pb)], writes=[str_t.b(sk)])
                mk_ap, mk_b = mask
                S.op('dve', lambda e, pk=pk, sk=sk, kw=kw, mk_ap=mk_ap: e.tensor_tensor(
                    out=p_t.s(pk, 0, w, p=kw), in0=str_t.s(sk, 0, w, p=kw), in1=mk_ap, op=ALU.mult),
                    reads=[str_t.b(sk), mk_b], writes=[p_t.b(pk)])
            S.op('pe', lambda e, pk=pk, kw=kw, vl=vl, i=i: e.matmul(
                P.s(obank, 0, w), lhsT=vl, rhs=p_t.s(pk, 0, w, p=kw), start=(i == 0), stop=(i == nchk - 1)),
                reads=[p_t.b(pk)] + vrd, writes=[P.b(obank)], signal=(i == nchk - 1))
            ak = accs[i % 3]
            S.op(aeng[i % 3], lambda e, pk=pk, kw=kw, ak=ak: e.tensor_tensor(
                out=acc.s(ak, 0, w, p=kw), in0=acc.s(ak, 0, w, p=kw), in1=p_t.s(pk, 0, w, p=kw), op=ALU.add),
                reads=[p_t.b(pk), acc.b(ak)], writes=[acc.b(ak)])
        for n_, ak in enumerate(accs):
            S.op('pe', lambda e, ak=ak, n_=n_: e.matmul(
                P.s(dbank, 0, w), lhsT=onesf.s(), rhs=acc.s(ak, 0, w), start=(n_ == 0), stop=(n_ == 2)),
                reads=[acc.b(ak), CONST], writes=[P.b(dbank)], signal=True)
        if den_extra is not None:
            S.op('dve', lambda e: e.tensor_scalar(out=rdt.s(0, 0, w), in0=P.s(dbank, 0, w), scalar1=den_extra,
                                                  scalar2=None, op0=ALU.add),
                 reads=[P.b(dbank), CONST], writes=[rdt.b()])
            S.op('dve', lambda e: e.reciprocal(out=rdt.s(0, 0, w), in_=rdt.s(0, 0, w)),
                 reads=[rdt.b()], writes=[rdt.b()])
        else:
            S.op('dve', lambda e: e.reciprocal(out=rdt.s(0, 0, w), in_=P.s(dbank, 0, w)),
                 reads=[P.b(dbank)], writes=[rdt.b()])
        ok = oring.next()
        S.op('dve', lambda e, ok=ok: e.tensor_tensor(out=ob_t.s(ok, 0, w), in0=P.s(obank, 0, w),
                                                     in1=rdt.s(0, 0, w), op=ALU.mult),
             reads=[P.b(obank), rdt.b()], writes=[ob_t.b(ok)])
        S.dma('pool', oTd[head * 128:(head + 1) * 128, o:o + w], ob_t.s(ok, 0, w), reads=[ob_t.b(ok)],
              writes=[DB('oTd', head, si)])

    def att_even(l, need_ctx):
        phase()
        e_ = l // 2
        XCH = cfg.XCH
        kt = AR.alloc(2, 4 * NT)
        vx = AR.alloc(2, 4 * XCH * 128)
        vc = AR.alloc(2, 4 * 128)
        ktl = AR.alloc(1, NT)
        vxl = AR.alloc(1, XCH * 128)
        qt = AR.alloc(3, 512)
        p_t = AR.alloc(8, 512)
        str_t = AR.alloc(2, 512)
        ob_t = AR.alloc(2, 512)
        mk = AR.alloc(14, 512)
        rdt = AFa.alloc(1, 512)
        acc = AFa.alloc(6, 512)
        accr = Ring(acc)
        qring, pring, strr, oring = Ring(qt), Ring(p_t), Ring(str_t), Ring(ob_t)
        sring = Ring(P, [0, 1, 2])
        obr, dbr = Ring(P, [3, 4]), Ring(P, [5, 6])
        S.dma('sp', mk.t[:, mk.off:mk.off + 14 * 512], msk[:, :], writes=[mk.b()])
        kvbuf = Ring(kt)
        scale = 128 ** -0.5
        for g in range(4):
            kb = kvbuf.next()
            S.dma('sp', kt.s(kb).rearrange("p (r n) -> p r n", r=4),
                  kTg[g].rearrange("(r p) n -> p r n", p=128), reads=[DB('kTg', g)], writes=[kt.b(kb)])
            vtg4 = vtg[g].rearrange("(r n) d -> r n d", r=4)
            for r in range(4):
                S.dma('sp', vx.s(kb, r * XCH * 128, (r + 1) * XCH * 128).rearrange("p (c d) -> p c d", c=XCH),
                      vtg4[r, 64:NT, :].rearrange("(c p) d -> p c d", p=128),
                      reads=[DB('vtg', g)], writes=[vx.b(kb)])
                S.dma('sp', vc.s(kb, r * 128, (r + 1) * 128, p=64), vtg4[r, 0:64, :],
                      reads=[DB('vtg', g)], writes=[vc.b(kb)])
            isB = g >= 2
            if isB:
                S.dma('sp', ktl.s(), kTb[g][:, :],
                      reads=[DB('kTb', g, si) for si in range(len(segs))], writes=[ktl.b()])
                S.dma('sp', vxl.s().rearrange("p (c d) -> p c d", c=XCH),
                      vtb[g][64:NT, :].rearrange("(c p) d -> p c d", p=128),
                      reads=[DB('vtb', g, si, tb) for si, (o, w, kd) in enumerate(segs) for tb in range(0, w, 128)],
                      writes=[vxl.b()])
            for hq in range(4):
                head = (8 if isB else 0) + (g % 2) * 4 + hq
                for si, (o, w, kd) in enumerate(segs):
                    if kd == 'c' and not need_ctx:
                        continue
                    qk = qring.next()
                    S.dma('sp', qt.s(qk, 0, w), qTd[head * 128:(head + 1) * 128, o:o + w],
                          reads=[DB('qTd', head, si)], writes=[qt.b(qk)])

                    def chunks_of(si_, kb=kb, qk=qk, w=w, kd=kd, isB=isB):
                        ch = []
                        qrhs = qt.s(qk, 0, w)

                        def ctxc(r):
                            return (64, [(kt.s(kb, r * NT, r * NT + 64), qrhs, [kt.b(kb), qt.b(qk)])],
                                    vc.s(kb, r * 128, (r + 1) * 128, p=64), [vc.b(kb)], None)

                        def gx(r, c, m=None):
                            k0 = r * NT + 64 + c * 128
                            v0 = (r * XCH + c) * 128
                            return (128, [(kt.s(kb, k0, k0 + 128), qrhs, [kt.b(kb), qt.b(qk)])],
                                    vx.s(kb, v0, v0 + 128), [vx.b(kb)],
                                    None if m is None else (mk.s(m, 0, w), mk.b()))

                        def lx(c, m):
                            k0 = 64 + c * 128
                            return (128, [(ktl.s(0, k0, k0 + 128), qrhs, [ktl.b(), qt.b(qk)])],
                                    vxl.s(0, c * 128, c * 128 + 128), [vxl.b()], (mk.s(m, 0, w), mk.b()))

                        for r in range(4):
                            ch.append(ctxc(r))
                        if kd == 'c':
                            return ch
                        if not isB:
                            for r in range(4):
                                for c in range(XCH):
                                    ch.append(gx(r, c))
                            return ch
                        sx = si_ - 1
                        for j in range(6):
                            c = 4 * sx - 1 + j
                            if 0 <= c < XCH:
                                ch.append(lx(c, j))
                        if sx == 0:
                            for r in range(4):
                                ch.append(gx(r, XCH - 1, 6 + r))
                        if sx == cfg.NXS - 1:
                            for r in range(4):
                                ch.append(gx(r, 0, 10 + r))
                        return ch

                    attend(None, chunks_of, None, esnkT.s(e_, head - 8, head - 7) if isB else None, scale, si, head,
                           oring, ob_t, strr, str_t, pring, p_t, rdt, sring, obr.next(), dbr.next(), acc, accr)
                    S.pump(2)

    def outproj_phase(l, need_ctx):
        phase()
        key = ('wo', l)
        wr = AR.alloc(2, 16 * 128)
        proj_residual.wr = wr
        proj_residual.wring = Ring(wr)
        proj_residual.load = _pr_load_full
        xr = AFa.alloc(3, 512)
        xo = AFa.alloc(2, 512)
        xring, xor_ = Ring(xr), Ring(xo)
        pyr = Ring(P, [0, 1])
        for grp in cfg.groups:
            sl = [si for si in grp if not (segs[si][2] == 'c' and not need_ctx)]
            if not sl:
                continue
            ot = AR.alloc(16, 1024)
            offs = {}
            oo = 0
            for si in sl:
                o, w, kd = segs[si]
                offs[si] = oo
                for hh in range(16):
                    S.dma('sp', ot.s(hh, oo, oo + w), oTd[hh * 128:(hh + 1) * 128, o:o + w],
                          reads=[DB('oTd', hh, si)], writes=[ot.b(hh, si)])
                oo += w
            proj_residual(l, 1, key, 16, lambda kk, si: ot.s(kk, offs[si], offs[si] + segs[si][1]), sl, pyr,
                          xring, xr, xor_, xo, lambda kk, si: [ot.b(kk, si)])

    def qkv_odd(l):
        phase()
        o_ = l // 2
        kci, kuq = ('cin', l), ('uq', l)
        a = AR.alloc(DC, 512)
        wr = AR.alloc(3, DC * 128)
        wq = AR.alloc(4, 512)
        sq = AR.alloc(3, 512)
        outb = AR.alloc(3, 512)
        cqn = AR.alloc(4, 512)
        xr = AFa.alloc(3, 512)
        tm = AFa.alloc(4, 512)
        cqs = AFa.alloc(4, 512)
        rstd = AFa.alloc(1, 512)
        rs = AFa.alloc(1, 512)
        cs = AFa.alloc(2, 512)
        xring, sqr, tmr, wring, outr, wqr = Ring(xr), Ring(sq), Ring(tm), Ring(wr), Ring(outb), Ring(wq)
        pmr, ssr = Ring(P, [0, 1, 2, 3]), Ring(P, [4])
        pqr = Ring(P, [5, 7])

        def inproj(u, w, si, col0=0, ncol=128):
            wk = wring.next()
            S.dma('sp', wr.s(wk), wunit(kci, u), reads=[wbuf(kci, u)], writes=[wr.b(wk)])
            pb = pmr.next()
            for dc in range(DC):
                S.op('pe', lambda e, pb=pb, wk=wk, dc=dc: e.matmul(
                    P.s(pb, 0, w, p=ncol), lhsT=wr.s(wk, dc * 128 + col0, dc * 128 + col0 + ncol),
                    rhs=a.s(dc, 0, w), start=(dc == 0), stop=(dc == DC - 1)),
                    reads=[wr.b(wk), a.b(dc, 0)], writes=[P.b(pb)], signal=(dc == DC - 1))
            return pb

        def chunk_norm(units, nfeat, gbase, si, w, dst_t):
            nb = len(units)
            ssb = ssr.next()
            for c, u in enumerate(units):
                pb = inproj(u, w, si)
                S.op('act', lambda e, pb=pb, c=c: e.activation(out=cqs.s(c, 0, w), in_=P.s(pb, 0, w),
                                                               func=AF.Identity),
                     reads=[P.b(pb)], writes=[cqs.b(c)])
                q = sqr.next()
                S.op('act', lambda e, pb=pb, q=q: e.activation(out=sq.s(q, 0, w), in_=P.s(pb, 0, w), func=AF.Square),
                     reads=[P.b(pb)], writes=[sq.b(q)])
                S.op('pe', lambda e, q=q, c=c: e.matmul(P.s(ssb, 0, w), lhsT=ones.s(), rhs=sq.s(q, 0, w),
                                                       start=(c == 0), stop=(c == nb - 1)),
                     reads=[sq.b(q), CONST], writes=[P.b(ssb)])
            S.op('act', lambda e: e.activation(out=rs.s(0, 0, w), in_=P.s(ssb, 0, w), func=AF.Sqrt,
                                               scale=1.0 / nfeat, bias=EPS), reads=[P.b(ssb)], writes=[rs.b()])
            S.op('dve', lambda e: e.reciprocal(out=rs.s(0, 0, w), in_=rs.s(0, 0, w)), reads=[rs.b()], writes=[rs.b()])
            for c in range(nb):
                S.op('dve', lambda e, c=c: e.scalar_tensor_tensor(
                    out=dst_t.s(c, 0, w), in0=cqs.s(c, 0, w), scalar=gcT.s(o_, gbase + c, gbase + c + 1),
                    in1=rs.s(0, 0, w), op0=ALU.mult, op1=ALU.mult),
                    reads=[cqs.b(c), rs.b(), CONST], writes=[dst_t.b(c)])

        for si, (o, w, kd) in enumerate(segs):
            norm_seg(l, 1, si, a, 0, xring, xr, sqr, sq, tmr, tm, rstd, 6)
            S.dma('sp', cs.s(0, 0, w, p=64), cs64[:, o:o + w], writes=[cs.b()])
            S.dma('sp', cs.s(1, 0, w, p=64), cs64[:, NT + o:NT + o + w], writes=[cs.b()])
            ckn = outb
            chunk_norm([4, 5], 256, 4, si, w, ckn)
            for c in range(2):
                S.dma('pool', lab[c][:, o:o + w], ckn.s(c, 0, w), reads=[ckn.b(c)],
                      writes=[DB('lab', c, si)])
            pk = inproj(6, w, si, 0, 64)
            pks = inproj(6, w, si, 64, 64)
            rope_store(pk, pks, w, o, cs, cs.b(), None, None, None, None, tm, tmr, outb, Ring(outb, [2]),
                       lab[2][:, o:o + w], DB('lab', 2, si), 64)
            chunk_norm([0, 1, 2, 3], 512, 0, si, w, cqn)
            for hh in range(16):
                wn, wrp = wqr.next(), wqr.next()
                S.dma('sp', wq.s(wn), wunit(kuq, 2 * hh), reads=[wbuf(kuq, 2 * hh)], writes=[wq.b(wn)])
                S.dma('sp', wq.s(wrp), wunit(kuq, 2 * hh + 1), reads=[wbuf(kuq, 2 * hh + 1)], writes=[wq.b(wrp)])
                pn = pqr.next()
                for c in range(4):
                    S.op('pe', lambda e, pn=pn, wn=wn, c=c, w=w: e.matmul(
                        P.s(pn, 0, w), lhsT=wq.s(wn, c * 128, c * 128 + 128), rhs=cqn.s(c, 0, w),
                        start=(c == 0), stop=(c == 3)), reads=[wq.b(wn), cqn.b(c)], writes=[P.b(pn)],
                        signal=(c == 3))
                ob = outr.next()
                S.op('act', lambda e, pn=pn, ob=ob, w=w: e.activation(out=outb.s(ob, 0, w), in_=P.s(pn, 0, w),
                                                                 func=AF.Identity),
                     reads=[P.b(pn)], writes=[outb.b(ob)])
                S.dma('pool', qTd[hh * 128:(hh + 1) * 128, o:o + w], outb.s(ob, 0, w), reads=[outb.b(ob)],
                      writes=[DB('qTd', hh, si)])
                prs = []
                for col0 in (0, 64):
                    pb = pmr.next()
                    prs.append(pb)
                    for c in range(4):
                        S.op('pe', lambda e, pb=pb, wrp=wrp, c=c, col0=col0, w=w: e.matmul(
                            P.s(pb, 0, w, p=64), lhsT=wq.s(wrp, c * 128 + col0, c * 128 + col0 + 64),
                            rhs=cqn.s(c, 0, w), start=(c == 0), stop=(c == 3)),
                            reads=[wq.b(wrp), cqn.b(c)], writes=[P.b(pb)], signal=(c == 3))
                rope_store(prs[0], prs[1], w, o, cs, cs.b(), None, None, None, None, tm, tmr, outb, outr,
                           qrd[hh * 64:(hh + 1) * 64, o:o + w], DB('qrd', hh, si), 64)
        for c in range(3):
            S.cc(lambda e, c=c: e.collective_compute("AllGather", ALU.bypass, replica_groups=GRP4,
                                                     ins=[lab[c][:, :]], outs=[lag[c][:, :]]),
                 reads=[DB('lab', c, si) for si in range(len(segs))], writes=[DB('lag', c)])

    def att_odd(l, need_ctx):
        phase()
        XCH = cfg.XCH
        kkk, kkv = ('ukk', l), ('ukv', l)
        ckg = AR.alloc(2, 4 * NT)
        krg = AR.alloc(1, 4 * NT)
        knt = AR.alloc(1, 4 * NT)
        vh = AR.alloc(1, 4 * (XCH + 1) * 128)
        wk_t = AR.alloc(2, 256)
        wv_t = AR.alloc(2, 256)
        qt = AR.alloc(3, 512)
        qr_t = AR.alloc(3, 512)
        p_t = AR.alloc(8, 512)
        ob_t = AR.alloc(2, 512)
        rdt = AFa.alloc(1, 512)
        acc = AFa.alloc(6, 512)
        accr = Ring(acc)
        qring, pring, oring, wkr, wvr = Ring(qt), Ring(p_t), Ring(ob_t), Ring(wk_t), Ring(wv_t)
        sring = Ring(P, [0, 1, 2])
        obr, dbr = Ring(P, [3, 4]), Ring(P, [5])
        upr = Ring(P, [6, 7])
        scale = 192 ** -0.5
        for c in range(2):
            S.dma('sp', ckg.s(c).rearrange("p (r n) -> p r n", r=4),
                  lag[c].rearrange("(r p) n -> p r n", p=128), reads=[DB('lag', c)], writes=[ckg.b()])
        S.op('dve', lambda e: e.memset(krg.s(), 0.0), writes=[krg.b()])
        for k_ in range(3):
            S.op('dve', lambda e, k_=k_: e.memset(qr_t.s(k_), 0.0), writes=[qr_t.b(k_)])
        S.dma('sp', krg.s(0, 0, 4 * NT, p=64).rearrange("p (r n) -> p r n", r=4),
              lag[2].rearrange("(r p) n -> p r n", p=64), reads=[DB('lag', 2)], writes=[krg.b()])
        for hh in range(16):
            wk, wv = wkr.next(), wvr.next()
            S.dma('sp', wk_t.s(wk), wunit(kkk, hh), reads=[wbuf(kkk, hh)], writes=[wk_t.b(wk)])
            S.dma('sp', wv_t.s(wv), wunit(kkv, hh), reads=[wbuf(kkv, hh)], writes=[wv_t.b(wv)])
            for c0 in range(0, 4 * NT, 512):
                cw = min(512, 4 * NT - c0)
                pb = upr.next()
                for c in range(2):
                    S.op('pe', lambda e, pb=pb, wk=wk, c=c, c0=c0, cw=cw: e.matmul(
                        P.s(pb, 0, cw), lhsT=wk_t.s(wk, c * 128, c * 128 + 128), rhs=ckg.s(c, c0, c0 + cw),
                        start=(c == 0), stop=(c == 1)), reads=[wk_t.b(wk), ckg.b()], writes=[P.b(pb)],
                        signal=(c == 1))
                S.op('act', lambda e, pb=pb, c0=c0, cw=cw: e.activation(out=knt.s(0, c0, c0 + cw), in_=P.s(pb, 0, cw),
                                                                        func=AF.Identity),
                     reads=[P.b(pb)], writes=[knt.b()])
            clist = []
            for r in range(4):
                clist.append((r * NT, 64, (r * (XCH + 1)) * 128))
                for c in range(XCH):
                    clist.append((r * NT + 64 + c * 128, 128, (r * (XCH + 1) + 1 + c) * 128))
            for g0 in range(0, len(clist), 4):
                pb = upr.next()
                grp = clist[g0:g0 + 4]
                for gi, (k0, kw, v0) in enumerate(grp):
                    for c in range(2):
                        S.op('pe', lambda e, pb=pb, gi=gi, k0=k0, kw=kw, c=c, wv=wv: e.matmul(
                            P.s(pb, gi * 128, gi * 128 + 128, p=kw), lhsT=ckg.s(c, k0, k0 + kw),
                            rhs=wv_t.s(wv, c * 128, c * 128 + 128), start=(c == 0), stop=(c == 1)),
                            reads=[wv_t.b(wv), ckg.b()], writes=[P.b(pb)],
                            signal=(c == 1 and gi == len(grp) - 1))
                vfirst, ng = grp[0][2], len(grp)
                assert all(grp[gi][2] == vfirst + gi * 128 for gi in range(ng))
                S.op('dve', lambda e, pb=pb, vfirst=vfirst, ng=ng: e.tensor_copy(
                    out=vh.s(0, vfirst, vfirst + ng * 128), in_=P.s(pb, 0, ng * 128)),
                    reads=[P.b(pb)], writes=[vh.b()])
            for si, (o, w, kd) in enumerate(segs):
                if kd == 'c' and not need_ctx:
                    continue
                qk = qring.next()
                S.dma('sp', qt.s(qk, 0, w), qTd[hh * 128:(hh + 1) * 128, o:o + w], reads=[DB('qTd', hh, si)],
                      writes=[qt.b(qk)])
                S.dma('sp', qr_t.s(qk, 0, w, p=64), qrd[hh * 64:(hh + 1) * 64, o:o + w], reads=[DB('qrd', hh, si)],
                      writes=[qr_t.b(qk)])

                def chunks_of(si_, qk=qk, w=w, kd=kd):
                    ch = []
                    rd = [knt.b(), krg.b(), qt.b(qk), qr_t.b(qk)]
                    for r in range(4):
                        its = [(r * NT, 64, (r * (XCH + 1)) * 128)]
                        if kd != 'c':
                            its += [(r * NT + 64 + c * 128, 128, (r * (XCH + 1) + 1 + c) * 128) for c in range(XCH)]
                        for (k0, kw, v0) in its:
                            ch.append((kw, [(knt.s(0, k0, k0 + kw), qt.s(qk, 0, w), rd),
                                            (krg.s(0, k0, k0 + kw), qr_t.s(qk, 0, w), rd)],
                                       vh.s(0, v0, v0 + 128, p=kw), [vh.b()], None))
                    return ch

                attend(None, chunks_of, None, None, scale, si, hh, oring, ob_t, None, None, pring, p_t, rdt, sring,
                       obr.next(), dbr.next(), acc, accr)
                S.pump(2)

    def final_phase():
        phase()
        sq = AR.alloc(3, 512)
        xr = AFa.alloc(4, 512)
        xo = AFa.alloc(3, 512)
        rstd = AFa.alloc(1, 512)
        xring, sqr, xor_ = Ring(xr), Ring(sq), Ring(xo)
        for si, (o, w, kd) in enumerate(segs):
            if kd == 'c':
                continue
            for dc in range(DC):
                k = xring.next()
                S.dma('sp', xr.s(k, 0, w), xres[dc * 128:(dc + 1) * 128, o:o + w],
                      reads=[DB('xres', si, dc)], writes=[xr.b(k)])
                q = sqr.next()
                S.op('act', lambda e, k=k, q=q, w=w: e.activation(out=sq.s(q, 0, w), in_=xr.s(k, 0, w), func=AF.Square),
                     reads=[xr.b(k)], writes=[sq.b(q)])
                S.op('pe', lambda e, q=q, dc=dc, w=w: e.matmul(P.s(6, 0, w), lhsT=ones.s(), rhs=sq.s(q, 0, w),
                                                         start=(dc == 0), stop=(dc == DC - 1)),
                     reads=[sq.b(q), CONST], writes=[P.b(6)])
            S.op('act', lambda e, w=w: e.activation(out=rstd.s(0, 0, w), in_=P.s(6, 0, w), func=AF.Sqrt,
                                               scale=1.0 / D, bias=EPS), reads=[P.b(6)], writes=[rstd.b()])
            S.op('dve', lambda e, w=w: e.reciprocal(out=rstd.s(0, 0, w), in_=rstd.s(0, 0, w)),
                 reads=[rstd.b()], writes=[rstd.b()])
            for dc in range(DC):
                k = xring.next()
                S.dma('sp', xr.s(k, 0, w), xres[dc * 128:(dc + 1) * 128, o:o + w],
                      reads=[DB('xres', si, dc)], writes=[xr.b(k)])
                ko = xor_.next()
                S.op('dve', lambda e, k=k, ko=ko, dc=dc, w=w: e.scalar_tensor_tensor(
                    out=xo.s(ko, 0, w), in0=xr.s(k, 0, w), scalar=gfT.s(0, dc, dc + 1), in1=rstd.s(0, 0, w),
                    op0=ALU.mult, op1=ALU.mult), reads=[xr.b(k), rstd.b(), CONST], writes=[xo.b(ko)])
                S.dma('pool', yT[dc * 128:(dc + 1) * 128, o - 64:o - 64 + w], xo.s(ko, 0, w), reads=[xo.b(ko)],
                      writes=[DB('yT', si, dc)])

    class _Stop(Exception):
        pass

    def ck(tag):
        if getattr(cfg, 'stop_after', None) == tag:
            raise _Stop()

    try:
        queue_weights(0, only=('gu0', 'dn0'))
        S.flush_bg()
        mod_phase()
        ck('mod')
        queue_weights(0, skip=('gu0', 'dn0'))
        S.flush_bg()
        ck('modw')
        for l in range(DEPTH):
            need_ctx = l < DEPTH - 1
            ffn_phase(l, 0, 0)
            ck(f'ffn0_{l}')
            if l % 2 == 0:
                qkv_even(l)
                ck(f'qkv_{l}')
                if l + 1 < DEPTH:
                    queue_weights(l + 1)
                att_even(l, need_ctx)
            else:
                qkv_odd(l)
                ck(f'qkv_{l}')
                if l + 1 < DEPTH:
                    queue_weights(l + 1)
                att_odd(l, need_ctx)
            S.flush_bg()
            ck(f'att_{l}')
            outproj_phase(l, need_ctx)
            ck(f'op_{l}')
            ffn_phase(l, 2, 1, skip_ctx=not need_ctx)
            S.flush_bg()
            ck(f'ffn1_{l}')
        final_phase()
    except _Stop:
        pass
    S.fence(full=True)

    semh = {}
    keys = list(Sched.ENG) + [k for lst in S.dsems.values() for k in lst] + ['cc']
    for k in keys:
        nm = k if isinstance(k, str) else f'{k[0]}{k[1]}'
        semh[k] = es.enter_context(nc.semaphore('s_' + nm))

    def replay(name, e):
        for (wl, fn, sk, inc) in S.streams[name]:
            for (k, v) in wl:
                e.wait_ge(semh[k], v)
            if fn is None:
                continue
            ins = fn(e)
            if sk is not None:
                if inc is None:
                    ins.then_inc(semh[sk])
                else:
                    ins.then_inc(semh[sk], inc)

    with nc.Block() as block:
        @block.tensor
        def _(e):
            replay('pe', e)

        @block.scalar
        def _(e):
            replay('act', e)

        @block.vector
        def _(e):
            replay('dve', e)

        @block.gpsimd
        def _(e):
            replay('pool', e)

        @block.sync
        def _(e):
            replay('sp', e)
    es.close()
    return nc, S


def _units(W, cols_list):
    K = W.shape[0]
    KC = K // 128
    out = []
    for cols in cols_list:
        if cols is None:
            out.append(None)
            continue
        sub = W[:, cols].reshape(KC, 128, len(cols)).transpose(1, 0, 2).reshape(128, KC * len(cols))
        out.append(sub)
    F = next(u.shape[1] for u in out if u is not None)
    out = [np.zeros((128, F), np.float32) if u is None else u for u in out]
    return np.stack(out, 0)


def _rope_tables(cfg, c4, rot):
    nf = rot // 4
    NT, TOK = cfg.NT, cfg.TOK
    t = np.arange(TOK) + c4 * TOK
    pos = np.stack([t // 64, t % 64], 0).astype(np.float32)
    inv = (np.float32(10000.0) ** (-np.arange(nf, dtype=np.float32) / np.float32(nf))).astype(np.float32)
    p = np.arange(rot)
    axis = p // (rot // 2)
    half = (p % (rot // 2)) // nf
    f = p % nf
    ang = pos[axis, :] * inv[f][:, None]
    cos = np.ones((rot, NT), np.float32)
    sin = np.zeros((rot, NT), np.float32)
    cos[:, 64:] = np.cos(ang)
    sgn = np.where(half == 0, -1.0, 1.0).astype(np.float32)[:, None]
    sin[:, 64:] = np.sin(ang) * sgn
    return np.ascontiguousarray(np.concatenate([cos, sin], axis=1))


def _masks(cfg, c4):
    i = np.arange(128)[:, None]
    n = np.arange(512)[None, :]
    M = np.zeros((14, 128, 512), np.float32)
    for j in range(6):
        M[j] = (np.abs(n - (j - 1) * 128 - i) <= 128)
    for r in range(4):
        if r == c4 - 1:
            M[6 + r] = M[0]
        if r == c4 + 1:
            M[10 + r] = M[5]
    return np.ascontiguousarray(M.transpose(1, 0, 2).reshape(128, 14 * 512)).astype(ml_dtypes.bfloat16)


def prep(cfg, inp):
    D, DC, FC, DFF, TOK, NT, DEPTH, PC, NCH = cfg.D, cfg.DC, cfg.FC, cfg.DFF, cfg.TOK, cfg.NT, cfg.DEPTH, cfg.PC, cfg.NCH
    NE, NO = cfg.NE, cfg.NO
    f32 = lambda a: np.asarray(a, dtype=np.float32)
    x, ctx = f32(inp['x']), f32(inp['ctx'])
    maps = [dict() for _ in range(8)]
    a128 = np.arange(128)
    a64 = np.arange(64)
    cvrows = np.stack([f32(inp['c'])[0], f32(inp['c'])[1], f32(inp['c_ctx'])], 0)
    cvt = np.ascontiguousarray(cvrows.reshape(3, DC, 128).transpose(2, 1, 0).reshape(128, DC * 3))
    gnt = np.ascontiguousarray(f32(inp['g_norm']).reshape(DEPTH, 3, DC, 128).transpose(3, 0, 1, 2).reshape(128, -1))
    gft = np.ascontiguousarray(f32(inp['g_final']).reshape(DC, 128).T)
    gq, gk = f32(inp['g_qnorm_a']), f32(inp['g_knorm_a'])
    gqk = np.stack([np.stack([gq[e], gq[e][a128 ^ 32], gk[e], gk[e][a128 ^ 32]], 1) for e in range(NE)], 1)
    gqk = np.ascontiguousarray(gqk.reshape(128, NE * 4))
    snk = np.ascontiguousarray(np.broadcast_to(f32(inp['sink_b']).reshape(1, NE * 8), (128, NE * 8)))
    if NO:
        gcq, gckv = f32(inp['g_cq']), f32(inp['g_ckv'])
        gcl = [np.concatenate([gcq[o].reshape(4, 128).T, gckv[o].reshape(2, 128).T], 1) for o in range(NO)]
        gct = np.ascontiguousarray(np.concatenate(gcl, 1))
    else:
        gct = np.zeros((128, 6), np.float32)
    wmod, bmod = f32(inp['w_mod']), f32(inp['b_mod'])
    wm_units = np.zeros((DEPTH, 4 * PC, 128, DC * 128), np.float32)
    bm_units = np.zeros((DEPTH, 4 * PC, 128), np.float32)
    for l in range(DEPTH):
        wm_units[l, :NCH] = wmod[l].reshape(DC, 128, NCH, 128).transpose(2, 1, 0, 3).reshape(NCH, 128, DC * 128)
        bm_units[l, :NCH] = bmod[l].reshape(NCH, 128)
    packs = {}
    wgu, wdn = f32(inp['w_gate_up']), f32(inp['w_down'])
    for l in range(DEPTH):
        for j in range(2):
            packs[(f'gu{j}', l)] = wgu[l, j].reshape(DC, 128, 2, FC, 128).transpose(3, 2, 1, 0, 4).reshape(
                2 * FC, 128, DC * 128)
            packs[(f'dn{j}', l)] = wdn[l, j].reshape(2, FC // 2, 128, DC, 128).transpose(3, 0, 2, 1, 4).reshape(
                2 * DC, 128, FC * 64)
        if l % 2 == 0:
            e = l // 2
            W = f32(inp['w_in_ab'])[e]
            cl = []
            for base, nh in ((0, 8), (1024, 2), (1536, 8), (2560, 2)):
                cl += [base + hh * 128 + a128 for hh in range(nh)]
                cl += [base + hh * 128 + (a128 ^ 32) for hh in range(nh)]
            packs[('ab', l)] = _units(W, cl)
            vcols = np.concatenate([1280 + np.arange(256), 2816 + np.arange(256)])
            packs[('v', l)] = np.ascontiguousarray(W[:, vcols]).reshape(DC, 128, 512)
            Wo = f32(inp['w_out_ab'])[e]
        else:
            o = l // 2
            W = f32(inp['w_in_c'])[o]
            cl = [c * 128 + a128 for c in range(4)] + [512 + c * 128 + a128 for c in range(2)]
            cl += [np.concatenate([768 + a64, 768 + (a64 ^ 16)]), None]
            packs[('cin', l)] = _units(W, cl)
            Wq = f32(inp['w_uq'])[o]
            cl = []
            for hh in range(16):
                cl.append(hh * 192 + a128)
                cl.append(np.concatenate([hh * 192 + 128 + a64, hh * 192 + 128 + (a64 ^ 16)]))
            packs[('uq', l)] = _units(Wq, cl)
            Wk = f32(inp['w_ukv'])[o]
            packs[('ukk', l)] = _units(Wk, [hh * 256 + a128 for hh in range(16)])
            packs[('ukv', l)] = _units(Wk, [hh * 256 + 128 + a128 for hh in range(16)])
            Wo = f32(inp['w_out_c'])[o]
        packs[('wo', l)] = Wo.reshape(16, 128, DC, 128).transpose(2, 1, 0, 3).reshape(DC, 128, 16 * 128)
    for core in range(8):
        b, c4 = divmod(core, 4)
        m = maps[core]
        xt = np.concatenate([ctx[b, c4 * 64:(c4 + 1) * 64], x[b, c4 * TOK:(c4 + 1) * TOK]], axis=0)
        m['xT'] = np.ascontiguousarray(xt.T)
        m['wm'] = np.ascontiguousarray(wm_units[:, c4 * PC:(c4 + 1) * PC]).reshape(DEPTH * PC * 128, DC * 128)
        bsh = bm_units[:, c4 * PC:(c4 + 1) * PC]
        m['bm'] = np.ascontiguousarray(np.repeat(bsh.transpose(2, 0, 1)[:, :, :, None], 3, axis=3)).reshape(128, -1)
        m['cv'] = cvt
        s = np.zeros((128, 2), np.float32)
        s[:, b] = 1.0
        m['sel'] = s
        m['gn'], m['gf'], m['gqk'], m['snk'], m['gc'] = gnt, gft, gqk, snk, gct
        m['cs128'] = _rope_tables(cfg, c4, 128)
        m['cs64'] = _rope_tables(cfg, c4, 64)
        m['msk'] = _masks(cfg, c4)
        for (name, l), pk in packs.items():
            U = pk.shape[0]
            m[f'w_{name}_{l}'] = np.ascontiguousarray(pk[c4 * U // 4:(c4 + 1) * U // 4]).reshape(-1, pk.shape[2])
    return maps


def run(cfg, inputs):
    maps = prep(cfg, inputs)
    nc, S = build(cfg)
    res = run_bass_kernel_spmd(nc, maps, core_ids=list(range(8)))
    out = np.zeros((2, cfg.SEQ, cfg.D), np.float32)
    for core in range(8):
        b, c4 = divmod(core, 4)
        out[b, c4 * cfg.TOK:(c4 + 1) * cfg.TOK, :] = np.asarray(res.results[core]['yT'], dtype=np.float32).T
    return out


def kernel(**inputs):
    return run(Cfg(), inputs)
```

```python
import numpy as np
import ml_dtypes
from contextlib import ExitStack
import concourse.bass as bass
import concourse.mybir as mybir
from concourse.bass_utils import run_bass_kernel_spmd

F32 = mybir.dt.float32
BF16 = mybir.dt.bfloat16
AF = mybir.ActivationFunctionType
ALU = mybir.AluOpType
EPS = 1e-6


class Cfg:
    def __init__(self, D=2048, DFF=5632, SEQ=8192, DEPTH=4):
        self.D, self.DFF, self.SEQ, self.DEPTH = D, DFF, SEQ, DEPTH
        self.DC, self.FC = D // 128, DFF // 128
        self.TOK = SEQ // 4
        self.CT = 64
        self.NT = self.TOK + self.CT
        self.NE, self.NO = (DEPTH + 1) // 2, DEPTH // 2
        self.NCH = 9 * self.DC
        self.PC = -(-self.NCH // 4)
        self.NXS = self.TOK // 512
        self.segs = [(0, 64, 'c')] + [(64 + i * 512, 512, 'x') for i in range(self.NXS)]
        ns = len(self.segs)
        self.groups = [list(range(i, min(i + 2, ns))) for i in range(0, ns, 2)]
        self.XCH = self.TOK // 128


class Buf:
    __slots__ = ('w', 'r')

    def __init__(self):
        self.w = {}
        self.r = {}


class Sched:
    ENG = ('pe', 'act', 'dve', 'pool', 'sp')

    def __init__(self, nds=16):
        self.streams = {e: [] for e in self.ENG}
        self.cnt = {e: 0 for e in self.ENG}
        self.waited = {e: {} for e in self.ENG}
        self.dsems = {'sp': [('dsp', i) for i in range(nds)], 'pool': [('dpl', i) for i in range(nds)],
                      'bg': [('dbg', i) for i in range(8)]}
        self.dval = {}
        for lst in self.dsems.values():
            for k in lst:
                self.dval[k] = 0
        self.drr = {'sp': 0, 'pool': 0, 'bg': 0}
        self.ccval = 0
        self.bg = []
        self.nops = 0

    def _waits(self, eng, reads, writes, extra=()):
        need = {}
        for b in reads:
            for k, v in b.w.items():
                if need.get(k, 0) < v:
                    need[k] = v
        for b in writes:
            for k, v in b.w.items():
                if need.get(k, 0) < v:
                    need[k] = v
            for k, v in b.r.items():
                if need.get(k, 0) < v:
                    need[k] = v
        for k, v in extra:
            if need.get(k, 0) < v:
                need[k] = v
        wl = []
        wd = self.waited[eng]
        for k, v in need.items():
            if k == eng and eng == 'pe':
                continue
            if wd.get(k, 0) >= v:
                continue
            wd[k] = v
            wl.append((k, v))
        return wl

    def _mark(self, ev, reads, writes):
        k, v = ev
        for b in reads:
            if b.r.get(k, 0) < v:
                b.r[k] = v
        for b in writes:
            b.w = {k: v}
            b.r = {}

    def op(self, eng, fn, reads=(), writes=(), signal=True):
        wl = self._waits(eng, reads, writes)
        if signal:
            self.cnt[eng] += 1
            ev = (eng, self.cnt[eng])
        else:
            ev = (eng, self.cnt[eng] + 1)
        self.streams[eng].append((wl, fn, eng if signal else None, 1))
        self._mark(ev, reads, writes)
        self.nops += 1

    def dma(self, eng, out, in_, reads=(), writes=(), pool=None):
        pool = pool or eng
        lst = self.dsems[pool]
        k = lst[self.drr[pool]]
        self.drr[pool] = (self.drr[pool] + 1) % len(lst)
        prev = self.dval[k]
        wl = self._waits(eng, reads, writes, extra=((k, prev),) if prev else ())
        self.dval[k] = prev + 16
        self.streams[eng].append((wl, (lambda e, o=out, i=in_: e.dma_start(out=o, in_=i)), k, 16))
        self._mark((k, prev + 16), reads, writes)
        self.nops += 1

    def cc(self, fn, reads=(), writes=()):
        wl = self._waits('pool', reads, writes)
        self.ccval += 1
        self.streams['pool'].append((wl, fn, 'cc', None))
        self._mark(('cc', self.ccval), reads, writes)

    def fence(self, full=False):
        tg = {e: c for e, c in self.cnt.items() if c > 0}
        for k, v in self.dval.items():
            if v > 0 and (full or k[0] != 'dbg'):
                tg[k] = v
        if self.ccval and full:
            tg['cc'] = self.ccval
        for e in self.ENG:
            wl = []
            wd = self.waited[e]
            for k, v in tg.items():
                if k == e:
                    continue
                if wd.get(k, 0) >= v:
                    continue
                wd[k] = v
                wl.append((k, v))
            if wl:
                self.streams[e].append((wl, None, None, 0))

    def pump(self, n=1):
        for _ in range(n):
            if not self.bg:
                return
            self.bg.pop(0)()

    def flush_bg(self):
        while self.bg:
            self.bg.pop(0)()


class Tl:
    def __init__(self, t, off, K, N):
        self.t, self.off, self.K, self.N = t, off, K, N
        self.bufs = {}

    def s(self, k=0, lo=0, hi=None, p=128, p0=0):
        hi = self.N if hi is None else hi
        c = self.off + k * self.N
        return self.t[p0:p0 + p, c + lo:c + hi]

    def b(self, *key):
        bb = self.bufs.get(key)
        if bb is None:
            bb = self.bufs[key] = Buf()
        return bb


class Arena:
    def __init__(self, t, n):
        self.t, self.n, self.off = t, n, 0

    def reset(self):
        self.off = 0

    def alloc(self, K, N):
        tl = Tl(self.t, self.off, K, N)
        self.off += K * N
        assert self.off <= self.n, (self.off, self.n)
        return tl


class Ring:
    def __init__(self, tl, idx=None):
        self.tl = tl
        self.idx = list(range(tl.K)) if idx is None else idx
        self.i = 0

    def next(self):
        k = self.idx[self.i]
        self.i = (self.i + 1) % len(self.idx)
        return k


class DBufs:
    def __init__(self):
        self.d = {}

    def __call__(self, *key):
        b = self.d.get(key)
        if b is None:
            b = self.d[key] = Buf()
        return b


def pack_defs(cfg, l):
    DC, FC = cfg.DC, cfg.FC
    d = {'gu0': (2 * FC, DC * 128), 'gu1': (2 * FC, DC * 128), 'dn0': (2 * DC, FC * 64), 'dn1': (2 * DC, FC * 64),
         'wo': (DC, 16 * 128)}
    if l % 2 == 0:
        d['ab'] = (40, DC * 128)
        d['v'] = (DC, 512)
    else:
        d['cin'] = (8, DC * 128)
        d['uq'] = (32, 512)
        d['ukk'] = (16, 256)
        d['ukv'] = (16, 256)
    for k, (U, F) in d.items():
        assert U % 4 == 0, (k, U)
    return d


def pieces_of(U, F):
    upr = U // 4
    k = max(1, min(upr, (1 << 20) // (128 * F * 2)))
    return [(u0, min(k, upr - u0)) for u0 in range(0, upr, k)]


def build(cfg):
    D, DC, FC, NT, TOK, DEPTH, PC, NCH = cfg.D, cfg.DC, cfg.FC, cfg.NT, cfg.TOK, cfg.DEPTH, cfg.PC, cfg.NCH
    NE, NO = cfg.NE, cfg.NO
    segs = cfg.segs
    nc = bass.Bass("TRN2", target_bir_lowering=False)
    S = Sched()
    es = ExitStack()
    DB = DBufs()

    def ein(name, shape, dt=F32):
        return nc.dram_tensor(name, list(shape), dt, kind="ExternalInput").ap()

    def dint(name, shape, dt=BF16):
        return nc.dram_tensor(name, list(shape), dt).ap()

    xT = ein('xT', [D, NT])
    wm = ein('wm', [DEPTH * PC * 128, DC * 128])
    bm = ein('bm', [128, DEPTH * PC * 3])
    cv = ein('cv', [128, DC * 3])
    sel = ein('sel', [128, 2])
    gn = ein('gn', [128, DEPTH * 3 * DC])
    gf = ein('gf', [128, DC])
    gqk = ein('gqk', [128, NE * 4])
    snk = ein('snk', [128, NE * 8])
    gc = ein('gc', [128, max(NO, 1) * 6])
    cs128 = ein('cs128', [128, 2 * NT])
    cs64 = ein('cs64', [64, 2 * NT])
    msk = ein('msk', [128, 14 * 512], BF16)
    yT = nc.dram_tensor('yT', [D, TOK], F32, kind="ExternalOutput").ap()
    wsh, wbn, wfull = {}, {}, {}
    for l in range(DEPTH):
        for name, (U, Fc) in pack_defs(cfg, l).items():
            key = (name, l)
            wsh[key] = ein(f'w_{name}_{l}', [U // 4 * 128, Fc])
            wbn[key] = [dint(f'b_{name}_{l}_{pi}', [n * 128, Fc]) for pi, (u0, n) in enumerate(pieces_of(U, Fc))]
            wfull[key] = [dint(f'f_{name}_{l}_{pi}', [4 * n * 128, Fc]) for pi, (u0, n) in enumerate(pieces_of(U, Fc))]
    xres = dint('xres', [D, NT], F32)
    modb = dint('modb', [128, DEPTH * PC * 3], F32)
    modg = dint('modg', [4 * 128, DEPTH * PC * 3], F32)
    qTd = dint('qTd', [16 * 128, NT])
    qrd = dint('qrd', [16 * 64, NT])
    oTd = dint('oTd', [16 * 128, NT])
    kTb = [dint(f'kTb{h}', [128, NT]) for h in range(4)]
    kTg = [dint(f'kTg{h}', [4 * 128, NT]) for h in range(4)]
    vtb = [dint(f'vtb{h}', [NT, 128]) for h in range(4)]
    vtg = [dint(f'vtg{h}', [4 * NT, 128]) for h in range(4)]
    lab = [dint('lab0', [128, NT]), dint('lab1', [128, NT]), dint('lab2', [64, NT])]
    lag = [dint('lag0', [4 * 128, NT]), dint('lag1', [4 * 128, NT]), dint('lag2', [4 * 64, NT])]
    GRP4 = [[0, 1, 2, 3], [4, 5, 6, 7]]

    AR_N = 83 * 1024
    AF_N = 7680
    ARt = es.enter_context(nc.sbuf_tensor('AR', [128, AR_N], BF16))
    AFt = es.enter_context(nc.sbuf_tensor('AFa', [128, AF_N], F32))
    CTn = 2 * DEPTH * NCH + 2 * 2 * DEPTH * 3 * DC + DEPTH * 3 * DC + DC + NE * 4 + 2 * NE * 8 + max(NO, 1) * 6 + 2 + 8
    CTt = es.enter_context(nc.sbuf_tensor('CT', [128, CTn], F32))
    ONt = es.enter_context(nc.sbuf_tensor('ON', [128, 128], BF16))
    ONFt = es.enter_context(nc.sbuf_tensor('ONF', [128, 128], F32))
    PSt = es.enter_context(nc.psum_tensor('PS', [128, 8 * 512], F32))
    AR = Arena(ARt, AR_N)
    AFa = Arena(AFt, AF_N)
    CT = Arena(CTt, CTn)
    P = Tl(PSt, 0, 8, 512)
    ones = Tl(ONt, 0, 1, 128)
    onesf = Tl(ONFt, 0, 1, 128)
    modL = CT.alloc(2 * DEPTH, NCH)
    gsT = CT.alloc(2 * DEPTH * 3, DC)
    gateT = CT.alloc(2 * DEPTH * 3, DC)
    gnT = CT.alloc(DEPTH * 3, DC)
    gfT = CT.alloc(1, DC)
    gqkT = CT.alloc(NE, 4)
    snkT = CT.alloc(NE, 8)
    esnkT = CT.alloc(NE, 8)
    gcT = CT.alloc(max(NO, 1), 6)
    selT = CT.alloc(1, 2)
    CONST = Buf()

    def phase():
        S.fence()
        AR.reset()
        AFa.reset()

    def mcol(kind, l, j, dc):
        return modL.s(kind * DEPTH + l, j * DC + dc, j * DC + dc + 1)

    def gscol(kind, l, sub, dc):
        return gsT.s((kind * DEPTH + l) * 3 + sub, dc, dc + 1)

    def gatecol(kind, l, sub, dc):
        return gateT.s((kind * DEPTH + l) * 3 + sub, dc, dc + 1)

    S.op('dve', lambda e: e.memset(ones.s(), 1.0), writes=[CONST])
    S.op('dve', lambda e: e.memset(onesf.s(), 1.0), writes=[CONST])
    for tl, src, n in ((gnT, gn, DEPTH * 3 * DC), (gfT, gf, DC), (gqkT, gqk, NE * 4), (snkT, snk, NE * 8),
                       (gcT, gc, max(NO, 1) * 6), (selT, sel, 2)):
        S.dma('sp', tl.t[:, tl.off:tl.off + n], src[:, :], writes=[CONST])
    for dc in range(DC):
        for si, (o, w, kd) in enumerate(segs):
            S.dma('pool', xres[dc * 128:(dc + 1) * 128, o:o + w], xT[dc * 128:(dc + 1) * 128, o:o + w],
                  writes=[DB('xres', si, dc)])
    S.op('act', lambda e: e.activation(out=esnkT.t[:, esnkT.off:esnkT.off + NE * 8],
                                       in_=snkT.t[:, snkT.off:snkT.off + NE * 8], func=AF.Exp),
         reads=[CONST], writes=[CONST])

    def mod_phase():
        U = DEPTH * PC
        cvt = AFa.alloc(1, DC * 3)
        bmt = AFa.alloc(1, U * 3)
        msh = AFa.alloc(1, U * 3)
        mg = AFa.alloc(1, 4 * U * 3)
        tmpx = AFa.alloc(1, 4 * U)
        sct = AR.alloc(DC, 3)
        wmr = AR.alloc(3, DC * 128)
        ring = Ring(wmr)
        S.dma('sp', cvt.s(), cv[:, :], writes=[cvt.b()])
        S.dma('sp', bmt.s(), bm[:, :], writes=[bmt.b()])
        S.op('act', lambda e: e.activation(out=sct.t[:, sct.off:sct.off + DC * 3], in_=cvt.s(), func=AF.Silu),
             reads=[cvt.b()], writes=[sct.b()])
        for u in range(U):
            k = ring.next()
            S.dma('pool', wmr.s(k), wm[u * 128:(u + 1) * 128, :], writes=[wmr.b(k)])
            for kc in range(DC):
                S.op('pe', lambda e, k=k, kc=kc, u=u: e.matmul(
                    P.s(0, u * 3, u * 3 + 3), lhsT=wmr.s(k, kc * 128, kc * 128 + 128), rhs=sct.s(kc),
                    start=(kc == 0), stop=(kc == DC - 1)),
                    reads=[wmr.b(k), sct.b()], writes=[P.b(0)], signal=(kc == DC - 1))
        S.op('dve', lambda e: e.tensor_tensor(out=msh.s(), in0=P.s(0, 0, U * 3), in1=bmt.s(), op=ALU.add),
             reads=[P.b(0), bmt.b()], writes=[msh.b()])
        S.dma('pool', modb[:, :], msh.s(), reads=[msh.b()], writes=[DB('modb')])
        S.cc(lambda e: e.collective_compute("AllGather", ALU.bypass, replica_groups=GRP4,
                                            ins=[modb[:, :]], outs=[modg[:, :]]),
             reads=[DB('modb')], writes=[DB('modg')])
        S.dma('sp', mg.s().rearrange("p (r n) -> p r n", r=4), modg.rearrange("(r p) n -> p r n", p=128),
              reads=[DB('modg')], writes=[mg.b()])
        mg3 = mg.s().rearrange("p (m r) -> p m r", r=3)
        S.op('dve', lambda e: e.tensor_scalar(out=tmpx.s(), in0=mg3[:, :, 0], scalar1=selT.s(0, 0, 1), scalar2=None,
                                              op0=ALU.mult), reads=[mg.b(), CONST], writes=[tmpx.b()])
        S.op('dve', lambda e: e.scalar_tensor_tensor(out=tmpx.s(), in0=mg3[:, :, 1], scalar=selT.s(0, 1, 2),
                                                     in1=tmpx.s(), op0=ALU.mult, op1=ALU.add),
             reads=[mg.b(), tmpx.b(), CONST], writes=[tmpx.b()])
        for r8 in range(4):
            c0 = r8 * PC
            n = min(PC, NCH - c0)
            if n <= 0:
                continue
            for l in range(DEPTH):
                o = (r8 * DEPTH + l) * PC
                S.op('dve', lambda e, l=l, c0=c0, n=n, o=o: e.tensor_copy(
                    out=modL.s(l, c0, c0 + n), in_=tmpx.s(0, o, o + n)), reads=[tmpx.b()], writes=[CONST])
                S.op('dve', lambda e, l=l, c0=c0, n=n, o=o: e.tensor_copy(
                    out=modL.s(DEPTH + l, c0, c0 + n), in_=mg3[:, o:o + n, 2]), reads=[mg.b()], writes=[CONST])
        for kind in range(2):
            for l in range(DEPTH):
                for sub in range(3):
                    r = (kind * DEPTH + l) * 3 + sub
                    ml = kind * DEPTH + l
                    S.op('dve', lambda e, r=r, ml=ml, sub=sub, l=l: e.scalar_tensor_tensor(
                        out=gsT.s(r), in0=modL.s(ml, (3 * sub + 1) * DC, (3 * sub + 2) * DC), scalar=1.0,
                        in1=gnT.s(l * 3 + sub), op0=ALU.add, op1=ALU.mult), reads=[CONST], writes=[CONST])
                    S.op('dve', lambda e, r=r, ml=ml, sub=sub: e.tensor_scalar(
                        out=gateT.s(r), in0=modL.s(ml, (3 * sub + 2) * DC, (3 * sub + 3) * DC),
                        scalar1=(1.0 if sub == 1 else 0.5), scalar2=None, op0=ALU.mult),
                        reads=[CONST], writes=[CONST])

    def queue_weights(l, only=None, skip=()):
        for name, (U, Fc) in pack_defs(cfg, l).items():
            if (only is not None and name not in only) or name in skip:
                continue
            key = (name, l)
            pcs = pieces_of(U, Fc)
            for pi, (u0, n) in enumerate(pcs):
                for j in range(n):
                    S.bg.append(lambda key=key, pi=pi, u0=u0, j=j: S.dma(
                        'pool', wbn[key][pi][j * 128:(j + 1) * 128, :], wsh[key][(u0 + j) * 128:(u0 + j + 1) * 128, :],
                        writes=[DB('wbn', key, pi, j)], pool='bg'))
            for pi, (u0, n) in enumerate(pcs):
                S.bg.append(lambda key=key, pi=pi, n=n: S.cc(
                    lambda e, key=key, pi=pi: e.collective_compute(
                        "AllGather", ALU.bypass, replica_groups=GRP4,
                        ins=[wbn[key][pi][:, :]], outs=[wfull[key][pi][:, :]]),
                    reads=[DB('wbn', key, pi, j) for j in range(n)], writes=[DB('wfull', key, pi)]))

    def wloc(key, u):
        U, Fc = pack_defs(cfg, key[1])[key[0]]
        upr = U // 4
        r, ul = divmod(u, upr)
        pcs = pieces_of(U, Fc)
        k = pcs[0][1]
        pi = ul // k
        n = pcs[pi][1]
        row = (r * n + (ul - pcs[pi][0])) * 128
        return pi, row

    def wunit(key, u):
        pi, row = wloc(key, u)
        return wfull[key][pi][row:row + 128, :]

    def wbuf(key, u):
        return DB('wfull', key, wloc(key, u)[0])

    def norm_seg(l, sub, si, a, aoff, xring, xr, sqr, sq, tmr, tm, rstd, statbank, slot=0):
        for _ in norm_seg_gen(l, sub, si, a, aoff, xring, xr, sqr, sq, tmr, tm, rstd, statbank, slot):
            pass

    def norm_seg_gen(l, sub, si, a, aoff, xring, xr, sqr, sq, tmr, tm, rstd, statbank, slot=0):
        o, w, kd = segs[si]
        kind = 0 if kd == 'x' else 1
        for dc in range(DC):
            k = xring.next()
            S.dma('sp', xr.s(k, 0, w), xres[dc * 128:(dc + 1) * 128, o:o + w],
                  reads=[DB('xres', si, dc)], writes=[xr.b(k)])
            q = sqr.next()
            S.op('act', lambda e, k=k, q=q: e.activation(out=sq.s(q, 0, w), in_=xr.s(k, 0, w), func=AF.Square),
                 reads=[xr.b(k)], writes=[sq.b(q)])
            S.op('pe', lambda e, q=q, dc=dc: e.matmul(P.s(statbank, 0, w), lhsT=ones.s(), rhs=sq.s(q, 0, w),
                                                     start=(dc == 0), stop=(dc == DC - 1)),
                 reads=[sq.b(q), CONST], writes=[P.b(statbank)])
            yield
        S.op('act', lambda e: e.activation(out=rstd.s(0, 0, w), in_=P.s(statbank, 0, w), func=AF.Sqrt,
                                           scale=1.0 / D, bias=EPS), reads=[P.b(statbank)], writes=[rstd.b()])
        S.op('dve', lambda e: e.reciprocal(out=rstd.s(0, 0, w), in_=rstd.s(0, 0, w)),
             reads=[rstd.b()], writes=[rstd.b()])
        for dc in range(DC):
            k = xring.next()
            S.dma('sp', xr.s(k, 0, w), xres[dc * 128:(dc + 1) * 128, o:o + w],
                  reads=[DB('xres', si, dc)], writes=[xr.b(k)])
            t = tmr.next()
            S.op('dve', lambda e, k=k, t=t, dc=dc: e.scalar_tensor_tensor(
                out=tm.s(t, 0, w), in0=xr.s(k, 0, w), scalar=gscol(kind, l, sub, dc), in1=rstd.s(0, 0, w),
                op0=ALU.mult, op1=ALU.mult), reads=[xr.b(k), rstd.b(), CONST], writes=[tm.b(t)])
            S.op('act', lambda e, t=t, dc=dc: e.activation(
                out=a.s(dc, aoff, aoff + w), in_=tm.s(t, 0, w), func=AF.Identity,
                bias=mcol(kind, l, 3 * sub, dc), scale=1.0), reads=[tm.b(t), CONST], writes=[a.b(dc, slot)])
            yield

    def _pr_load_full(key, oc, wr, wk, nk):
        S.dma('sp', wr.s(wk, 0, nk * 128), wunit(key, oc), reads=[wbuf(key, oc)], writes=[wr.b(wk)])

    def _pr_load_halves(key, oc, wr, wk, nk):
        hk = nk // 2
        for hf in range(2):
            S.dma('sp', wr.s(wk, hf * hk * 128, (hf + 1) * hk * 128), wunit(key, 2 * oc + hf),
                  reads=[wbuf(key, 2 * oc + hf)], writes=[wr.b(wk)])

    def proj_residual(l, sub, key, nk, rhs_of, seg_list, psr, xring, xr, xor_, xo, rd_bufs_of, gen=None, gsteps=0):
        wr = proj_residual.wr
        wring = proj_residual.wring
        for oc in range(DC):
            wk = wring.next()
            proj_residual.load(key, oc, wr, wk, nk)
            for si in seg_list:
                o, w, kd = segs[si]
                kind = 0 if kd == 'x' else 1
                pb = psr.next()
                for kk in range(nk):
                    rhs_ap = rhs_of(kk, si)
                    S.op('pe', lambda e, pb=pb, wk=wk, kk=kk, si=si, w=w, rhs_ap=rhs_ap, wr=wr: e.matmul(
                        P.s(pb, 0, w), lhsT=wr.s(wk, kk * 128, kk * 128 + 128), rhs=rhs_ap,
                        start=(kk == 0), stop=(kk == nk - 1)),
                        reads=[wr.b(wk)] + rd_bufs_of(kk, si), writes=[P.b(pb)], signal=(kk == nk - 1))
                k = xring.next()
                S.dma('sp', xr.s(k, 0, w), xres[oc * 128:(oc + 1) * 128, o:o + w],
                      reads=[DB('xres', si, oc)], writes=[xr.b(k)])
                ko = xor_.next()
                S.op('dve', lambda e, pb=pb, k=k, ko=ko, w=w, kind=kind, oc=oc: e.scalar_tensor_tensor(
                    out=xo.s(ko, 0, w), in0=P.s(pb, 0, w), scalar=gatecol(kind, l, sub, oc), in1=xr.s(k, 0, w),
                    op0=ALU.mult, op1=ALU.add), reads=[P.b(pb), xr.b(k), CONST], writes=[xo.b(ko)])
                S.dma('pool', xres[oc * 128:(oc + 1) * 128, o:o + w], xo.s(ko, 0, w),
                      reads=[xo.b(ko)], writes=[DB('xres', si, oc)])
                if gen is not None:
                    for _ in range(gsteps):
                        next(gen, None)
            S.pump(2)
        if gen is not None:
            for _ in gen:
                pass

    def ffn_phase(l, sub, widx, skip_ctx=False):
        phase()
        GW = 1024
        a = AR.alloc(DC, GW)
        h = AR.alloc(FC, GW)
        gur = AR.alloc(4, DC * 128)
        dnr = AR.alloc(2, FC * 128)
        sq = AR.alloc(3, 512)
        xr = AFa.alloc(4, 512)
        xo = AFa.alloc(2, 512)
        tm = AFa.alloc(3, 512)
        rstd = AFa.alloc(1, 512)
        xring, sqr, tmr, xor_ = Ring(xr), Ring(sq), Ring(tm), Ring(xo)
        guring = Ring(gur)
        proj_residual.wr = dnr
        proj_residual.wring = Ring(dnr)
        proj_residual.load = _pr_load_halves
        pgr, pur, pyr = Ring(P, [0, 1]), Ring(P, [2, 3]), Ring(P, [4, 5])
        kgu, kdn = (f'gu{widx}', l), (f'dn{widx}', l)
        glist = []
        for grp in cfg.groups:
            sl = [si for si in grp if not (skip_ctx and segs[si][2] == 'c')]
            if sl:
                glist.append(sl)

        def norm_group_gen(sl):
            for n_, si in enumerate(sl):
                yield from norm_seg_gen(l, sub, si, a, n_ * 512, xring, xr, sqr, sq, tmr, tm, rstd, 6, n_)

        for _ in norm_group_gen(glist[0]):
            pass
        for gi, sl in enumerate(glist):
            offs = {si: n_ * 512 for n_, si in enumerate(sl)}
            slot = {si: n_ for n_, si in enumerate(sl)}
            for fc in range(FC):
                wg, wu = guring.next(), guring.next()
                S.dma('sp', gur.s(wg), wunit(kgu, 2 * fc), reads=[wbuf(kgu, 2 * fc)], writes=[gur.b(wg)])
                S.dma('sp', gur.s(wu), wunit(kgu, 2 * fc + 1), reads=[wbuf(kgu, 2 * fc + 1)], writes=[gur.b(wu)])
                for si in sl:
                    w = segs[si][1]
                    ao = offs[si]
                    pg, pu = pgr.next(), pur.next()
                    for pb, wk in ((pg, wg), (pu, wu)):
                        for dc in range(DC):
                            S.op('pe', lambda e, pb=pb, wk=wk, dc=dc, ao=ao, w=w: e.matmul(
                                P.s(pb, 0, w), lhsT=gur.s(wk, dc * 128, dc * 128 + 128), rhs=a.s(dc, ao, ao + w),
                                start=(dc == 0), stop=(dc == DC - 1)),
                                reads=[gur.b(wk), a.b(dc, slot[si])], writes=[P.b(pb)], signal=(dc == DC - 1))
                    t = tmr.next()
                    S.op('act', lambda e, t=t, pg=pg, w=w: e.activation(out=tm.s(t, 0, w), in_=P.s(pg, 0, w),
                                                                        func=AF.Silu),
                         reads=[P.b(pg)], writes=[tm.b(t)])
                    S.op('dve', lambda e, t=t, pu=pu, w=w, fc=fc, ao=ao: e.tensor_tensor(
                        out=h.s(fc, ao, ao + w), in0=tm.s(t, 0, w), in1=P.s(pu, 0, w), op=ALU.mult),
                        reads=[tm.b(t), P.b(pu)], writes=[h.b(fc, slot[si])])
                S.pump(1)
            gen = norm_group_gen(glist[gi + 1]) if gi + 1 < len(glist) else None
            nsteps = DC * len(sl)
            gst = 0 if gen is None else -(-(len(glist[gi + 1]) * (2 * DC + 1)) // nsteps)
            proj_residual(l, sub, kdn, FC, lambda kk, si: h.s(kk, offs[si], offs[si] + segs[si][1]), sl, pyr,
                          xring, xr, xor_, xo, lambda kk, si: [h.b(kk, slot[si])], gen=gen, gsteps=gst)

    def rope_store(psq, psqs, w, o, cs, csb, gcol, gscol_, rsb, rs, tm, tmr, outb, outr, dst, dstbuf, npart):
        t1, t2 = tmr.next(), tmr.next()
        if gcol is not None:
            S.op('dve', lambda e: e.scalar_tensor_tensor(out=tm.s(t1, 0, w, p=npart), in0=P.s(psq, 0, w, p=npart),
                                                         scalar=gcol, in1=cs.s(0, 0, w, p=npart), op0=ALU.mult,
                                                         op1=ALU.mult),
                 reads=[P.b(psq), csb, CONST], writes=[tm.b(t1)])
            S.op('dve', lambda e: e.scalar_tensor_tensor(out=tm.s(t2, 0, w, p=npart), in0=P.s(psqs, 0, w, p=npart),
                                                         scalar=gscol_, in1=cs.s(1, 0, w, p=npart), op0=ALU.mult,
                                                         op1=ALU.mult),
                 reads=[P.b(psqs), csb, CONST], writes=[tm.b(t2)])
        else:
            S.op('dve', lambda e: e.tensor_tensor(out=tm.s(t1, 0, w, p=npart), in0=P.s(psq, 0, w, p=npart),
                                                  in1=cs.s(0, 0, w, p=npart), op=ALU.mult),
                 reads=[P.b(psq), csb], writes=[tm.b(t1)])
            S.op('dve', lambda e: e.tensor_tensor(out=tm.s(t2, 0, w, p=npart), in0=P.s(psqs, 0, w, p=npart),
                                                  in1=cs.s(1, 0, w, p=npart), op=ALU.mult),
                 reads=[P.b(psqs), csb], writes=[tm.b(t2)])
        ob = outr.next()
        if rs is not None:
            S.op('dve', lambda e: e.tensor_tensor(out=tm.s(t1, 0, w, p=npart), in0=tm.s(t1, 0, w, p=npart),
                                                  in1=tm.s(t2, 0, w, p=npart), op=ALU.add),
                 reads=[tm.b(t1), tm.b(t2)], writes=[tm.b(t1)])
            S.op('dve', lambda e: e.tensor_tensor(out=outb.s(ob, 0, w, p=npart), in0=tm.s(t1, 0, w, p=npart),
                                                  in1=rs.s(0, 0, w, p=npart), op=ALU.mult),
                 reads=[tm.b(t1), rsb], writes=[outb.b(ob)])
        else:
            S.op('dve', lambda e: e.tensor_tensor(out=outb.s(ob, 0, w, p=npart), in0=tm.s(t1, 0, w, p=npart),
                                                  in1=tm.s(t2, 0, w, p=npart), op=ALU.add),
                 reads=[tm.b(t1), tm.b(t2)], writes=[outb.b(ob)])
        S.dma('pool', dst, outb.s(ob, 0, w, p=npart), reads=[outb.b(ob)], writes=[dstbuf])

    def head_rs(psq, w, nfeat, sq, sqr, ssbank, rs):
        q = sqr.next()
        S.op('act', lambda e: e.activation(out=sq.s(q, 0, w), in_=P.s(psq, 0, w), func=AF.Square),
             reads=[P.b(psq)], writes=[sq.b(q)])
        S.op('pe', lambda e: e.matmul(P.s(ssbank, 0, w), lhsT=ones.s(), rhs=sq.s(q, 0, w), start=True, stop=True),
             reads=[sq.b(q), CONST], writes=[P.b(ssbank)])
        S.op('act', lambda e: e.activation(out=rs.s(0, 0, w), in_=P.s(ssbank, 0, w), func=AF.Sqrt,
                                           scale=1.0 / nfeat, bias=EPS), reads=[P.b(ssbank)], writes=[rs.b()])
        S.op('dve', lambda e: e.reciprocal(out=rs.s(0, 0, w), in_=rs.s(0, 0, w)), reads=[rs.b()], writes=[rs.b()])

    def qkv_even(l):
        phase()
        e_ = l // 2
        kab, kv = ('ab', l), ('v', l)
        a = AR.alloc(DC, 512)
        wr = AR.alloc(4, DC * 128)
        wv = AR.alloc(DC, 512)
        sq = AR.alloc(3, 512)
        outb = AR.alloc(3, 512)
        xr = AFa.alloc(3, 512)
        tm = AFa.alloc(4, 512)
        rstd = AFa.alloc(1, 512)
        rs = AFa.alloc(1, 512)
        cs = AFa.alloc(2, 512)
        xring, sqr, tmr, wring, outr = Ring(xr), Ring(sq), Ring(tm), Ring(wr), Ring(outb)
        pqr, pqsr, ssr, pvr = Ring(P, [0, 1]), Ring(P, [2, 3]), Ring(P, [4]), Ring(P, [5, 7])
        S.dma('sp', wv.t[:, wv.off:wv.off + DC * 512].rearrange("p (k n) -> p k n", k=DC),
              wfull[kv][0].rearrange("(k p) n -> p k n", p=128), reads=[DB('wfull', kv, 0)], writes=[wv.b()])
        for si, (o, w, kd) in enumerate(segs):
            norm_seg(l, 1, si, a, 0, xring, xr, sqr, sq, tmr, tm, rstd, 6)
            S.dma('sp', cs.s(0, 0, w), cs128[:, o:o + w], writes=[cs.b()])
            S.dma('sp', cs.s(1, 0, w), cs128[:, NT + o:NT + o + w], writes=[cs.b()])
            jobs = []
            for hh in range(8):
                jobs.append((hh, 8 + hh, 0, qTd, hh * 128, ('qTd', hh, si)))
            for hh in range(2):
                jobs.append((16 + hh, 18 + hh, 2, kTb[hh], 0, ('kTb', hh, si)))
            for hh in range(8):
                jobs.append((20 + hh, 28 + hh, None, qTd, (8 + hh) * 128, ('qTd', 8 + hh, si)))
            for hh in range(2):
                jobs.append((36 + hh, 38 + hh, None, kTb[2 + hh], 0, ('kTb', 2 + hh, si)))
            for (u0, u1, gi, dst, drow, dkey) in jobs:
                pbs = []
                for u, rg in ((u0, pqr), (u1, pqsr)):
                    wk = wring.next()
                    S.dma('sp', wr.s(wk), wunit(kab, u), reads=[wbuf(kab, u)], writes=[wr.b(wk)])
                    pb = rg.next()
                    pbs.append(pb)
                    for dc in range(DC):
                        S.op('pe', lambda e, pb=pb, wk=wk, dc=dc, w=w: e.matmul(
                            P.s(pb, 0, w), lhsT=wr.s(wk, dc * 128, dc * 128 + 128), rhs=a.s(dc, 0, w),
                            start=(dc == 0), stop=(dc == DC - 1)),
                            reads=[wr.b(wk), a.b(dc, 0)], writes=[P.b(pb)], signal=(dc == DC - 1))
                if gi is not None:
                    head_rs(pbs[0], w, 128, sq, sqr, ssr.next(), rs)
                    rope_store(pbs[0], pbs[1], w, o, cs, cs.b(), gqkT.s(e_, gi, gi + 1), gqkT.s(e_, gi + 1, gi + 2),
                               rs.b(), rs, tm, tmr, outb, outr, dst[drow:drow + 128, o:o + w], DB(*dkey), 128)
                else:
                    rope_store(pbs[0], pbs[1], w, o, cs, cs.b(), None, None, None, None, tm, tmr, outb, outr,
                               dst[drow:drow + 128, o:o + w], DB(*dkey), 128)
            for tb in range(0, w, 128):
                tw = min(128, w - tb)
                pb = pvr.next()
                for dc in range(DC):
                    S.op('pe', lambda e, pb=pb, dc=dc, tb=tb, tw=tw: e.matmul(
                        P.s(pb, 0, 512, p=tw), lhsT=a.s(dc, tb, tb + tw), rhs=wv.s(dc),
                        start=(dc == 0), stop=(dc == DC - 1)),
                        reads=[wv.b(), a.b(dc, 0)], writes=[P.b(pb)], signal=(dc == DC - 1))
                ob = outr.next()
                S.op('act', lambda e, pb=pb, ob=ob, tw=tw: e.activation(out=outb.s(ob, 0, 512, p=tw),
                                                                       in_=P.s(pb, 0, 512, p=tw), func=AF.Identity),
                     reads=[P.b(pb)], writes=[outb.b(ob)])
                for hv in range(4):
                    S.dma('pool', vtb[hv][o + tb:o + tb + tw, :], outb.s(ob, hv * 128, hv * 128 + 128, p=tw),
                          reads=[outb.b(ob)], writes=[DB('vtb', hv, si, tb)])
        for hv in range(4):
            S.cc(lambda e, hv=hv: e.collective_compute("AllGather", ALU.bypass, replica_groups=GRP4,
                                                       ins=[kTb[hv][:, :]], outs=[kTg[hv][:, :]]),
                 reads=[DB('kTb', hv, si) for si in range(len(segs))], writes=[DB('kTg', hv)])
            S.cc(lambda e, hv=hv: e.collective_compute("AllGather", ALU.bypass, replica_groups=GRP4,
                                                       ins=[vtb[hv][:, :]], outs=[vtg[hv][:, :]]),
                 reads=[DB('vtb', hv, si, tb) for si, (o, w, kd) in enumerate(segs) for tb in range(0, w, 128)],
                 writes=[DB('vtg', hv)])

    def attend(q_loads, chunks_of, pv_of, den_extra, scale, si, head, oring, ob_t, strr, str_t, pring, p_t,
               rdt, sbank_ring, obank, dbank, acc=None, accr=None, pe_den=False):
        o, w, kd = segs[si]
        chunks = chunks_of(si)
        nchk = len(chunks)
        sb = [None] * nchk

        def emit_qk(i):
            kw, qk_ops, _, _, _ = chunks[i]
            pb = sbank_ring.next()
            sb[i] = pb
            n = len(qk_ops)
            for j, (lhsT, rhs, rd) in enumerate(qk_ops):
                S.op('pe', lambda e, pb=pb, lhsT=lhsT, rhs=rhs, j=j, n=n, kw=kw: e.matmul(
                    P.s(pb, 0, w, p=kw), lhsT=lhsT, rhs=rhs, start=(j == 0), stop=(j == n - 1)),
                    reads=rd, writes=[P.b(pb)], signal=(j == n - 1))

        a0, a1 = accr.next(), accr.next()
        accs = (a0, a1)
        for n_, ak in enumerate(accs):
            S.op(('dve', 'pool')[n_], lambda e, ak=ak: e.memset(acc.s(ak, 0, w), 0.0), writes=[acc.b(ak)])
        dstate = [False]
        emit_qk(0)
        if nchk > 1:
            emit_qk(1)
        for i in range(nchk):
            if i + 2 < nchk:
                emit_qk(i + 2)
            kw, _, vl, vrd, mask = chunks[i]
            pb = sb[i]
            pk = pring.next()
            if mask is None:
                S.op('act', lambda e, pb=pb, pk=pk, kw=kw: e.activation(out=p_t.s(pk, 0, w, p=kw),
                                                                      in_=P.s(pb, 0, w, p=kw), func=AF.Exp,
                                                                      scale=scale),
                     reads=[P.b(pb)], writes=[p_t.b(pk)])
            else:
                sk = strr.next()
                S.op('act', lambda e, pb=pb, sk=sk, kw=kw: e.activation(out=str_t.s(sk, 0, w, p=kw),
                                                                      in_=P.s(pb, 0, w, p=kw), func=AF.Exp,
                                                                      scale=scale),
                     reads=[P.b(pb)], writes=[str_t.b(sk)])
                mk_ap, mk_b = mask
                S.op('dve', lambda e, pk=pk, sk=sk, kw=kw, mk_ap=mk_ap: e.tensor_tensor(
                    out=p_t.s(pk, 0, w, p=kw), in0=str_t.s(sk, 0, w, p=kw), in1=mk_ap, op=ALU.mult),
                    reads=[str_t.b(sk), mk_b], writes=[p_t.b(pk)])
            S.op('pe', lambda e, pk=pk, kw=kw, vl=vl, i=i: e.matmul(
                P.s(obank, 0, w), lhsT=vl, rhs=p_t.s(pk, 0, w, p=kw), start=(i == 0), stop=(i == nchk - 1)),
                reads=[p_t.b(pk)] + vrd, writes=[P.b(obank)], signal=(i == nchk - 1))
            if pe_den and i % 3 == 2:
                S.op('pe', lambda e, pk=pk, kw=kw, first=(not dstate[0]): e.matmul(
                    P.s(dbank, 0, w), lhsT=ones.s(0, 0, 128, p=kw), rhs=p_t.s(pk, 0, w, p=kw),
                    start=first, stop=False),
                    reads=[p_t.b(pk), CONST], writes=[P.b(dbank)], signal=True)
                dstate[0] = True
            else:
                j_ = (i % 3) if pe_den else (i % 2)
                ak = accs[j_]
                S.op(('dve', 'pool')[j_], lambda e, pk=pk, kw=kw, ak=ak: e.tensor_tensor(
                    out=acc.s(ak, 0, w, p=kw), in0=acc.s(ak, 0, w, p=kw), in1=p_t.s(pk, 0, w, p=kw), op=ALU.add),
                    reads=[p_t.b(pk), acc.b(ak)], writes=[acc.b(ak)])
        for n_, ak in enumerate(accs):
            S.op('pe', lambda e, ak=ak, n_=n_, first=(not dstate[0]): e.matmul(
                P.s(dbank, 0, w), lhsT=onesf.s(), rhs=acc.s(ak, 0, w), start=(first and n_ == 0), stop=(n_ == 1)),
                reads=[acc.b(ak), CONST], writes=[P.b(dbank)], signal=True)
        if den_extra is not None:
            S.op('dve', lambda e: e.tensor_scalar(out=rdt.s(0, 0, w), in0=P.s(dbank, 0, w), scalar1=den_extra,
                                                  scalar2=None, op0=ALU.add),
                 reads=[P.b(dbank), CONST], writes=[rdt.b()])
            S.op('dve', lambda e: e.reciprocal(out=rdt.s(0, 0, w), in_=rdt.s(0, 0, w)),
                 reads=[rdt.b()], writes=[rdt.b()])
        else:
            S.op('dve', lambda e: e.reciprocal(out=rdt.s(0, 0, w), in_=P.s(dbank, 0, w)),
                 reads=[P.b(dbank)], writes=[rdt.b()])
        ok = oring.next()
        S.op('dve', lambda e, ok=ok: e.tensor_tensor(out=ob_t.s(ok, 0, w), in0=P.s(obank, 0, w),
                                                     in1=rdt.s(0, 0, w), op=ALU.mult),
             reads=[P.b(obank), rdt.b()], writes=[ob_t.b(ok)])
        S.dma('pool', oTd[head * 128:(head + 1) * 128, o:o + w], ob_t.s(ok, 0, w), reads=[ob_t.b(ok)],
              writes=[DB('oTd', head, si)])

    def att_even(l, need_ctx):
        phase()
        e_ = l // 2
        XCH = cfg.XCH
        kt = AR.alloc(2, 4 * NT)
        vx = AR.alloc(2, 4 * XCH * 128)
        vc = AR.alloc(2, 4 * 128)
        ktl = AR.alloc(1, NT)
        vxl = AR.alloc(1, XCH * 128)
        qt = AR.alloc(3, 512)
        p_t = AR.alloc(4, 512)
        str_t = AR.alloc(2, 512)
        ob_t = AR.alloc(2, 512)
        mk = AR.alloc(14, 512)
        rdt = AFa.alloc(1, 512)
        acc = AFa.alloc(4, 512)
        accr = Ring(acc)
        qring, pring, strr, oring = Ring(qt), Ring(p_t), Ring(str_t), Ring(ob_t)
        sring = Ring(P, [0, 1, 2])
        obr, dbr = Ring(P, [3, 4]), Ring(P, [5, 6])
        S.dma('sp', mk.t[:, mk.off:mk.off + 14 * 512], msk[:, :], writes=[mk.b()])
        kvbuf = Ring(kt)
        scale = 128 ** -0.5
        for g in range(4):
            kb = kvbuf.next()
            S.dma('sp', kt.s(kb).rearrange("p (r n) -> p r n", r=4),
                  kTg[g].rearrange("(r p) n -> p r n", p=128), reads=[DB('kTg', g)], writes=[kt.b(kb)])
            vtg4 = vtg[g].rearrange("(r n) d -> r n d", r=4)
            for r in range(4):
                S.dma('sp', vx.s(kb, r * XCH * 128, (r + 1) * XCH * 128).rearrange("p (c d) -> p c d", c=XCH),
                      vtg4[r, 64:NT, :].rearrange("(c p) d -> p c d", p=128),
                      reads=[DB('vtg', g)], writes=[vx.b(kb)])
                S.dma('sp', vc.s(kb, r * 128, (r + 1) * 128, p=64), vtg4[r, 0:64, :],
                      reads=[DB('vtg', g)], writes=[vc.b(kb)])
            isB = g >= 2
            if isB:
                S.dma('sp', ktl.s(), kTb[g][:, :],
                      reads=[DB('kTb', g, si) for si in range(len(segs))], writes=[ktl.b()])
                S.dma('sp', vxl.s().rearrange("p (c d) -> p c d", c=XCH),
                      vtb[g][64:NT, :].rearrange("(c p) d -> p c d", p=128),
                      reads=[DB('vtb', g, si, tb) for si, (o, w, kd) in enumerate(segs) for tb in range(0, w, 128)],
                      writes=[vxl.b()])
            for hq in range(4):
                head = (8 if isB else 0) + (g % 2) * 4 + hq
                for si, (o, w, kd) in enumerate(segs):
                    if kd == 'c' and not need_ctx:
                        continue
                    qk = qring.next()
                    S.dma('sp', qt.s(qk, 0, w), qTd[head * 128:(head + 1) * 128, o:o + w],
                          reads=[DB('qTd', head, si)], writes=[qt.b(qk)])

                    def chunks_of(si_, kb=kb, qk=qk, w=w, kd=kd, isB=isB):
                        ch = []
                        qrhs = qt.s(qk, 0, w)

                        def ctxc(r):
                            return (64, [(kt.s(kb, r * NT, r * NT + 64), qrhs, [kt.b(kb), qt.b(qk)])],
                                    vc.s(kb, r * 128, (r + 1) * 128, p=64), [vc.b(kb)], None)

                        def gx(r, c, m=None):
                            k0 = r * NT + 64 + c * 128
                            v0 = (r * XCH + c) * 128
                            return (128, [(kt.s(kb, k0, k0 + 128), qrhs, [kt.b(kb), qt.b(qk)])],
                                    vx.s(kb, v0, v0 + 128), [vx.b(kb)],
                                    None if m is None else (mk.s(m, 0, w), mk.b()))

                        def lx(c, m):
                            k0 = 64 + c * 128
                            return (128, [(ktl.s(0, k0, k0 + 128), qrhs, [ktl.b(), qt.b(qk)])],
                                    vxl.s(0, c * 128, c * 128 + 128), [vxl.b()], (mk.s(m, 0, w), mk.b()))

                        for r in range(4):
                            ch.append(ctxc(r))
                        if kd == 'c':
                            return ch
                        if not isB:
                            for r in range(4):
                                for c in range(XCH):
                                    ch.append(gx(r, c))
                            return ch
                        sx = si_ - 1
                        for j in range(6):
                            c = 4 * sx - 1 + j
                            if 0 <= c < XCH:
                                ch.append(lx(c, j))
                        if sx == 0:
                            for r in range(4):
                                ch.append(gx(r, XCH - 1, 6 + r))
                        if sx == cfg.NXS - 1:
                            for r in range(4):
                                ch.append(gx(r, 0, 10 + r))
                        return ch

                    attend(None, chunks_of, None, esnkT.s(e_, head - 8, head - 7) if isB else None, scale, si, head,
                           oring, ob_t, strr, str_t, pring, p_t, rdt, sring, obr.next(), dbr.next(), acc, accr,
                           pe_den=(not isB))
                    S.pump(2)

    def outproj_phase(l, need_ctx):
        phase()
        key = ('wo', l)
        wr = AR.alloc(2, 16 * 128)
        proj_residual.wr = wr
        proj_residual.wring = Ring(wr)
        proj_residual.load = _pr_load_full
        xr = AFa.alloc(3, 512)
        xo = AFa.alloc(2, 512)
        xring, xor_ = Ring(xr), Ring(xo)
        pyr = Ring(P, [0, 1])
        for grp in cfg.groups:
            sl = [si for si in grp if not (segs[si][2] == 'c' and not need_ctx)]
            if not sl:
                continue
            ot = AR.alloc(16, 1024)
            offs = {}
            oo = 0
            for si in sl:
                o, w, kd = segs[si]
                offs[si] = oo
                for hh in range(16):
                    S.dma('sp', ot.s(hh, oo, oo + w), oTd[hh * 128:(hh + 1) * 128, o:o + w],
                          reads=[DB('oTd', hh, si)], writes=[ot.b(hh, si)])
                oo += w
            proj_residual(l, 1, key, 16, lambda kk, si: ot.s(kk, offs[si], offs[si] + segs[si][1]), sl, pyr,
                          xring, xr, xor_, xo, lambda kk, si: [ot.b(kk, si)])

    def qkv_odd(l):
        phase()
        o_ = l // 2
        kci, kuq = ('cin', l), ('uq', l)
        a = AR.alloc(DC, 512)
        wr = AR.alloc(3, DC * 128)
        wq = AR.alloc(4, 512)
        sq = AR.alloc(3, 512)
        outb = AR.alloc(3, 512)
        cqn = AR.alloc(4, 512)
        xr = AFa.alloc(3, 512)
        tm = AFa.alloc(4, 512)
        cqs = AFa.alloc(4, 512)
        rstd = AFa.alloc(1, 512)
        rs = AFa.alloc(1, 512)
        cs = AFa.alloc(2, 512)
        xring, sqr, tmr, wring, outr, wqr = Ring(xr), Ring(sq), Ring(tm), Ring(wr), Ring(outb), Ring(wq)
        pmr, ssr = Ring(P, [0, 1, 2, 3]), Ring(P, [4])
        pqr = Ring(P, [5, 7])

        def inproj(u, w, si, col0=0, ncol=128):
            wk = wring.next()
            S.dma('sp', wr.s(wk), wunit(kci, u), reads=[wbuf(kci, u)], writes=[wr.b(wk)])
            pb = pmr.next()
            for dc in range(DC):
                S.op('pe', lambda e, pb=pb, wk=wk, dc=dc: e.matmul(
                    P.s(pb, 0, w, p=ncol), lhsT=wr.s(wk, dc * 128 + col0, dc * 128 + col0 + ncol),
                    rhs=a.s(dc, 0, w), start=(dc == 0), stop=(dc == DC - 1)),
                    reads=[wr.b(wk), a.b(dc, 0)], writes=[P.b(pb)], signal=(dc == DC - 1))
            return pb

        def chunk_norm(units, nfeat, gbase, si, w, dst_t):
            nb = len(units)
            ssb = ssr.next()
            for c, u in enumerate(units):
                pb = inproj(u, w, si)
                S.op('act', lambda e, pb=pb, c=c: e.activation(out=cqs.s(c, 0, w), in_=P.s(pb, 0, w),
                                                               func=AF.Identity),
                     reads=[P.b(pb)], writes=[cqs.b(c)])
                q = sqr.next()
                S.op('act', lambda e, pb=pb, q=q: e.activation(out=sq.s(q, 0, w), in_=P.s(pb, 0, w), func=AF.Square),
                     reads=[P.b(pb)], writes=[sq.b(q)])
                S.op('pe', lambda e, q=q, c=c: e.matmul(P.s(ssb, 0, w), lhsT=ones.s(), rhs=sq.s(q, 0, w),
                                                       start=(c == 0), stop=(c == nb - 1)),
                     reads=[sq.b(q), CONST], writes=[P.b(ssb)])
            S.op('act', lambda e: e.activation(out=rs.s(0, 0, w), in_=P.s(ssb, 0, w), func=AF.Sqrt,
                                               scale=1.0 / nfeat, bias=EPS), reads=[P.b(ssb)], writes=[rs.b()])
            S.op('dve', lambda e: e.reciprocal(out=rs.s(0, 0, w), in_=rs.s(0, 0, w)), reads=[rs.b()], writes=[rs.b()])
            for c in range(nb):
                S.op('dve', lambda e, c=c: e.scalar_tensor_tensor(
                    out=dst_t.s(c, 0, w), in0=cqs.s(c, 0, w), scalar=gcT.s(o_, gbase + c, gbase + c + 1),
                    in1=rs.s(0, 0, w), op0=ALU.mult, op1=ALU.mult),
                    reads=[cqs.b(c), rs.b(), CONST], writes=[dst_t.b(c)])

        for si, (o, w, kd) in enumerate(segs):
            norm_seg(l, 1, si, a, 0, xring, xr, sqr, sq, tmr, tm, rstd, 6)
            S.dma('sp', cs.s(0, 0, w, p=64), cs64[:, o:o + w], writes=[cs.b()])
            S.dma('sp', cs.s(1, 0, w, p=64), cs64[:, NT + o:NT + o + w], writes=[cs.b()])
            ckn = outb
            chunk_norm([4, 5], 256, 4, si, w, ckn)
            for c in range(2):
                S.dma('pool', lab[c][:, o:o + w], ckn.s(c, 0, w), reads=[ckn.b(c)],
                      writes=[DB('lab', c, si)])
            pk = inproj(6, w, si, 0, 64)
            pks = inproj(6, w, si, 64, 64)
            rope_store(pk, pks, w, o, cs, cs.b(), None, None, None, None, tm, tmr, outb, Ring(outb, [2]),
                       lab[2][:, o:o + w], DB('lab', 2, si), 64)
            chunk_norm([0, 1, 2, 3], 512, 0, si, w, cqn)
            for hh in range(16):
                wn, wrp = wqr.next(), wqr.next()
                S.dma('sp', wq.s(wn), wunit(kuq, 2 * hh), reads=[wbuf(kuq, 2 * hh)], writes=[wq.b(wn)])
                S.dma('sp', wq.s(wrp), wunit(kuq, 2 * hh + 1), reads=[wbuf(kuq, 2 * hh + 1)], writes=[wq.b(wrp)])
                pn = pqr.next()
                for c in range(4):
                    S.op('pe', lambda e, pn=pn, wn=wn, c=c, w=w: e.matmul(
                        P.s(pn, 0, w), lhsT=wq.s(wn, c * 128, c * 128 + 128), rhs=cqn.s(c, 0, w),
                        start=(c == 0), stop=(c == 3)), reads=[wq.b(wn), cqn.b(c)], writes=[P.b(pn)],
                        signal=(c == 3))
                ob = outr.next()
                S.op('act', lambda e, pn=pn, ob=ob, w=w: e.activation(out=outb.s(ob, 0, w), in_=P.s(pn, 0, w),
                                                                 func=AF.Identity),
                     reads=[P.b(pn)], writes=[outb.b(ob)])
                S.dma('pool', qTd[hh * 128:(hh + 1) * 128, o:o + w], outb.s(ob, 0, w), reads=[outb.b(ob)],
                      writes=[DB('qTd', hh, si)])
                prs = []
                for col0 in (0, 64):
                    pb = pmr.next()
                    prs.append(pb)
                    for c in range(4):
                        S.op('pe', lambda e, pb=pb, wrp=wrp, c=c, col0=col0, w=w: e.matmul(
                            P.s(pb, 0, w, p=64), lhsT=wq.s(wrp, c * 128 + col0, c * 128 + col0 + 64),
                            rhs=cqn.s(c, 0, w), start=(c == 0), stop=(c == 3)),
                            reads=[wq.b(wrp), cqn.b(c)], writes=[P.b(pb)], signal=(c == 3))
                rope_store(prs[0], prs[1], w, o, cs, cs.b(), None, None, None, None, tm, tmr, outb, outr,
                           qrd[hh * 64:(hh + 1) * 64, o:o + w], DB('qrd', hh, si), 64)
        for c in range(3):
            S.cc(lambda e, c=c: e.collective_compute("AllGather", ALU.bypass, replica_groups=GRP4,
                                                     ins=[lab[c][:, :]], outs=[lag[c][:, :]]),
                 reads=[DB('lab', c, si) for si in range(len(segs))], writes=[DB('lag', c)])

    def att_odd(l, need_ctx):
        phase()
        XCH = cfg.XCH
        kkk, kkv = ('ukk', l), ('ukv', l)
        ckg = AR.alloc(2, 4 * NT)
        krg = AR.alloc(1, 4 * NT)
        knt = AR.alloc(1, 4 * NT)
        vh = AR.alloc(1, 4 * (XCH + 1) * 128)
        wk_t = AR.alloc(2, 256)
        wv_t = AR.alloc(2, 256)
        qt = AR.alloc(3, 512)
        qr_t = AR.alloc(3, 512)
        p_t = AR.alloc(4, 512)
        ob_t = AR.alloc(2, 512)
        rdt = AFa.alloc(1, 512)
        acc = AFa.alloc(4, 512)
        accr = Ring(acc)
        qring, pring, oring, wkr, wvr = Ring(qt), Ring(p_t), Ring(ob_t), Ring(wk_t), Ring(wv_t)
        sring = Ring(P, [0, 1, 2])
        obr, dbr = Ring(P, [3, 4]), Ring(P, [5])
        upr = Ring(P, [6, 7])
        scale = 192 ** -0.5
        for c in range(2):
            S.dma('sp', ckg.s(c).rearrange("p (r n) -> p r n", r=4),
                  lag[c].rearrange("(r p) n -> p r n", p=128), reads=[DB('lag', c)], writes=[ckg.b()])
        S.op('dve', lambda e: e.memset(krg.s(), 0.0), writes=[krg.b()])
        for k_ in range(3):
            S.op('dve', lambda e, k_=k_: e.memset(qr_t.s(k_), 0.0), writes=[qr_t.b(k_)])
        S.dma('sp', krg.s(0, 0, 4 * NT, p=64).rearrange("p (r n) -> p r n", r=4),
              lag[2].rearrange("(r p) n -> p r n", p=64), reads=[DB('lag', 2)], writes=[krg.b()])
        for hh in range(16):
            wk, wv = wkr.next(), wvr.next()
            S.dma('sp', wk_t.s(wk), wunit(kkk, hh), reads=[wbuf(kkk, hh)], writes=[wk_t.b(wk)])
            S.dma('sp', wv_t.s(wv), wunit(kkv, hh), reads=[wbuf(kkv, hh)], writes=[wv_t.b(wv)])
            for c0 in range(0, 4 * NT, 512):
                cw = min(512, 4 * NT - c0)
                pb = upr.next()
                for c in range(2):
                    S.op('pe', lambda e, pb=pb, wk=wk, c=c, c0=c0, cw=cw: e.matmul(
                        P.s(pb, 0, cw), lhsT=wk_t.s(wk, c * 128, c * 128 + 128), rhs=ckg.s(c, c0, c0 + cw),
                        start=(c == 0), stop=(c == 1)), reads=[wk_t.b(wk), ckg.b()], writes=[P.b(pb)],
                        signal=(c == 1))
                S.op('act', lambda e, pb=pb, c0=c0, cw=cw: e.activation(out=knt.s(0, c0, c0 + cw), in_=P.s(pb, 0, cw),
                                                                        func=AF.Identity),
                     reads=[P.b(pb)], writes=[knt.b()])
            clist = []
            for r in range(4):
                clist.append((r * NT, 64, (r * (XCH + 1)) * 128))
                for c in range(XCH):
                    clist.append((r * NT + 64 + c * 128, 128, (r * (XCH + 1) + 1 + c) * 128))
            for g0 in range(0, len(clist), 4):
                pb = upr.next()
                grp = clist[g0:g0 + 4]
                for gi, (k0, kw, v0) in enumerate(grp):
                    for c in range(2):
                        S.op('pe', lambda e, pb=pb, gi=gi, k0=k0, kw=kw, c=c, wv=wv: e.matmul(
                            P.s(pb, gi * 128, gi * 128 + 128, p=kw), lhsT=ckg.s(c, k0, k0 + kw),
                            rhs=wv_t.s(wv, c * 128, c * 128 + 128), start=(c == 0), stop=(c == 1)),
                            reads=[wv_t.b(wv), ckg.b()], writes=[P.b(pb)],
                            signal=(c == 1 and gi == len(grp) - 1))
                vfirst, ng = grp[0][2], len(grp)
                assert all(grp[gi][2] == vfirst + gi * 128 for gi in range(ng))
                S.op('dve', lambda e, pb=pb, vfirst=vfirst, ng=ng: e.tensor_copy(
                    out=vh.s(0, vfirst, vfirst + ng * 128), in_=P.s(pb, 0, ng * 128)),
                    reads=[P.b(pb)], writes=[vh.b()])
            for si, (o, w, kd) in enumerate(segs):
                if kd == 'c' and not need_ctx:
                    continue
                qk = qring.next()
                S.dma('sp', qt.s(qk, 0, w), qTd[hh * 128:(hh + 1) * 128, o:o + w], reads=[DB('qTd', hh, si)],
                      writes=[qt.b(qk)])
                S.dma('sp', qr_t.s(qk, 0, w, p=64), qrd[hh * 64:(hh + 1) * 64, o:o + w], reads=[DB('qrd', hh, si)],
                      writes=[qr_t.b(qk)])

                def chunks_of(si_, qk=qk, w=w, kd=kd):
                    ch = []
                    rd = [knt.b(), krg.b(), qt.b(qk), qr_t.b(qk)]
                    for r in range(4):
                        its = [(r * NT, 64, (r * (XCH + 1)) * 128)]
                        if kd != 'c':
                            its += [(r * NT + 64 + c * 128, 128, (r * (XCH + 1) + 1 + c) * 128) for c in range(XCH)]
                        for (k0, kw, v0) in its:
                            ch.append((kw, [(knt.s(0, k0, k0 + kw), qt.s(qk, 0, w), rd),
                                            (krg.s(0, k0, k0 + kw), qr_t.s(qk, 0, w), rd)],
                                       vh.s(0, v0, v0 + 128, p=kw), [vh.b()], None))
                    return ch

                attend(None, chunks_of, None, None, scale, si, hh, oring, ob_t, None, None, pring, p_t, rdt, sring,
                       obr.next(), dbr.next(), acc, accr)
                S.pump(2)

    def final_phase():
        phase()
        sq = AR.alloc(3, 512)
        xr = AFa.alloc(4, 512)
        xo = AFa.alloc(3, 512)
        rstd = AFa.alloc(1, 512)
        xring, sqr, xor_ = Ring(xr), Ring(sq), Ring(xo)
        for si, (o, w, kd) in enumerate(segs):
            if kd == 'c':
                continue
            for dc in range(DC):
                k = xring.next()
                S.dma('sp', xr.s(k, 0, w), xres[dc * 128:(dc + 1) * 128, o:o + w],
                      reads=[DB('xres', si, dc)], writes=[xr.b(k)])
                q = sqr.next()
                S.op('act', lambda e, k=k, q=q, w=w: e.activation(out=sq.s(q, 0, w), in_=xr.s(k, 0, w), func=AF.Square),
                     reads=[xr.b(k)], writes=[sq.b(q)])
                S.op('pe', lambda e, q=q, dc=dc, w=w: e.matmul(P.s(6, 0, w), lhsT=ones.s(), rhs=sq.s(q, 0, w),
                                                         start=(dc == 0), stop=(dc == DC - 1)),
                     reads=[sq.b(q), CONST], writes=[P.b(6)])
            S.op('act', lambda e, w=w: e.activation(out=rstd.s(0, 0, w), in_=P.s(6, 0, w), func=AF.Sqrt,
                                               scale=1.0 / D, bias=EPS), reads=[P.b(6)], writes=[rstd.b()])
            S.op('dve', lambda e, w=w: e.reciprocal(out=rstd.s(0, 0, w), in_=rstd.s(0, 0, w)),
                 reads=[rstd.b()], writes=[rstd.b()])
            for dc in range(DC):
                k = xring.next()
                S.dma('sp', xr.s(k, 0, w), xres[dc * 128:(dc + 1) * 128, o:o + w],
                      reads=[DB('xres', si, dc)], writes=[xr.b(k)])
                ko = xor_.next()
                S.op('dve', lambda e, k=k, ko=ko, dc=dc, w=w: e.scalar_tensor_tensor(
                    out=xo.s(ko, 0, w), in0=xr.s(k, 0, w), scalar=gfT.s(0, dc, dc + 1), in1=rstd.s(0, 0, w),
                    op0=ALU.mult, op1=ALU.mult), reads=[xr.b(k), rstd.b(), CONST], writes=[xo.b(ko)])
                S.dma('pool', yT[dc * 128:(dc + 1) * 128, o - 64:o - 64 + w], xo.s(ko, 0, w), reads=[xo.b(ko)],
                      writes=[DB('yT', si, dc)])

    class _Stop(Exception):
        pass

    def ck(tag):
        if getattr(cfg, 'stop_after', None) == tag:
            raise _Stop()

    try:
        queue_weights(0, only=('gu0', 'dn0'))
        S.flush_bg()
        mod_phase()
        ck('mod')
        queue_weights(0, skip=('gu0', 'dn0'))
        S.flush_bg()
        ck('modw')
        for l in range(DEPTH):
            need_ctx = l < DEPTH - 1
            ffn_phase(l, 0, 0)
            ck(f'ffn0_{l}')
            if l % 2 == 0:
                qkv_even(l)
                ck(f'qkv_{l}')
                if l + 1 < DEPTH:
                    queue_weights(l + 1)
                att_even(l, need_ctx)
            else:
                qkv_odd(l)
                ck(f'qkv_{l}')
                if l + 1 < DEPTH:
                    queue_weights(l + 1)
                att_odd(l, need_ctx)
            S.flush_bg()
            ck(f'att_{l}')
            outproj_phase(l, need_ctx)
            ck(f'op_{l}')
            ffn_phase(l, 2, 1, skip_ctx=not need_ctx)
            S.flush_bg()
            ck(f'ffn1_{l}')
        final_phase()
    except _Stop:
        pass
    S.fence(full=True)

    semh = {}
    keys = list(Sched.ENG) + [k for lst in S.dsems.values() for k in lst] + ['cc']
    for k in keys:
        nm = k if isinstance(k, str) else f'{k[0]}{k[1]}'
        semh[k] = es.enter_context(nc.semaphore('s_' + nm))

    def replay(name, e):
        for (wl, fn, sk, inc) in S.streams[name]:
            for (k, v) in wl:
                e.wait_ge(semh[k], v)
            if fn is None:
                continue
            ins = fn(e)
            if sk is not None:
                if inc is None:
                    ins.then_inc(semh[sk])
                else:
                    ins.then_inc(semh[sk], inc)

    with nc.Block() as block:
        @block.tensor
        def _(e):
            replay('pe', e)

        @block.scalar
        def _(e):
            replay('act', e)

        @block.vector
        def _(e):
            replay('dve', e)

        @block.gpsimd
        def _(e):
            replay('pool', e)

        @block.sync
        def _(e):
            replay('sp', e)
    es.close()
    return nc, S


def _units(W, cols_list):
    K = W.shape[0]
    KC = K // 128
    out = []
    for cols in cols_list:
        if cols is None:
            out.append(None)
            continue
        sub = W[:, cols].reshape(KC, 128, len(cols)).transpose(1, 0, 2).reshape(128, KC * len(cols))
        out.append(sub)
    F = next(u.shape[1] for u in out if u is not None)
    out = [np.zeros((128, F), np.float32) if u is None else u for u in out]
    return np.stack(out, 0)


def _rope_tables(cfg, c4, rot):
    nf = rot // 4
    NT, TOK = cfg.NT, cfg.TOK
    t = np.arange(TOK) + c4 * TOK
    pos = np.stack([t // 64, t % 64], 0).astype(np.float32)
    inv = (np.float32(10000.0) ** (-np.arange(nf, dtype=np.float32) / np.float32(nf))).astype(np.float32)
    p = np.arange(rot)
    axis = p // (rot // 2)
    half = (p % (rot // 2)) // nf
    f = p % nf
    ang = pos[axis, :] * inv[f][:, None]
    cos = np.ones((rot, NT), np.float32)
    sin = np.zeros((rot, NT), np.float32)
    cos[:, 64:] = np.cos(ang)
    sgn = np.where(half == 0, -1.0, 1.0).astype(np.float32)[:, None]
    sin[:, 64:] = np.sin(ang) * sgn
    return np.ascontiguousarray(np.concatenate([cos, sin], axis=1))


def _masks(cfg, c4):
    i = np.arange(128)[:, None]
    n = np.arange(512)[None, :]
    M = np.zeros((14, 128, 512), np.float32)
    for j in range(6):
        M[j] = (np.abs(n - (j - 1) * 128 - i) <= 128)
    for r in range(4):
        if r == c4 - 1:
            M[6 + r] = M[0]
        if r == c4 + 1:
            M[10 + r] = M[5]
    return np.ascontiguousarray(M.transpose(1, 0, 2).reshape(128, 14 * 512)).astype(ml_dtypes.bfloat16)


def prep(cfg, inp):
    D, DC, FC, DFF, TOK, NT, DEPTH, PC, NCH = cfg.D, cfg.DC, cfg.FC, cfg.DFF, cfg.TOK, cfg.NT, cfg.DEPTH, cfg.PC, cfg.NCH
    NE, NO = cfg.NE, cfg.NO
    f32 = lambda a: np.asarray(a, dtype=np.float32)
    x, ctx = f32(inp['x']), f32(inp['ctx'])
    maps = [dict() for _ in range(8)]
    a128 = np.arange(128)
    a64 = np.arange(64)
    cvrows = np.stack([f32(inp['c'])[0], f32(inp['c'])[1], f32(inp['c_ctx'])], 0)
    cvt = np.ascontiguousarray(cvrows.reshape(3, DC, 128).transpose(2, 1, 0).reshape(128, DC * 3))
    gnt = np.ascontiguousarray(f32(inp['g_norm']).reshape(DEPTH, 3, DC, 128).transpose(3, 0, 1, 2).reshape(128, -1))
    gft = np.ascontiguousarray(f32(inp['g_final']).reshape(DC, 128).T)
    gq, gk = f32(inp['g_qnorm_a']), f32(inp['g_knorm_a'])
    gqk = np.stack([np.stack([gq[e], gq[e][a128 ^ 32], gk[e], gk[e][a128 ^ 32]], 1) for e in range(NE)], 1)
    gqk = np.ascontiguousarray(gqk.reshape(128, NE * 4))
    snk = np.ascontiguousarray(np.broadcast_to(f32(inp['sink_b']).reshape(1, NE * 8), (128, NE * 8)))
    if NO:
        gcq, gckv = f32(inp['g_cq']), f32(inp['g_ckv'])
        gcl = [np.concatenate([gcq[o].reshape(4, 128).T, gckv[o].reshape(2, 128).T], 1) for o in range(NO)]
        gct = np.ascontiguousarray(np.concatenate(gcl, 1))
    else:
        gct = np.zeros((128, 6), np.float32)
    wmod, bmod = f32(inp['w_mod']), f32(inp['b_mod'])
    wm_units = np.zeros((DEPTH, 4 * PC, 128, DC * 128), np.float32)
    bm_units = np.zeros((DEPTH, 4 * PC, 128), np.float32)
    for l in range(DEPTH):
        wm_units[l, :NCH] = wmod[l].reshape(DC, 128, NCH, 128).transpose(2, 1, 0, 3).reshape(NCH, 128, DC * 128)
        bm_units[l, :NCH] = bmod[l].reshape(NCH, 128)
    packs = {}
    wgu, wdn = f32(inp['w_gate_up']), f32(inp['w_down'])
    for l in range(DEPTH):
        for j in range(2):
            packs[(f'gu{j}', l)] = wgu[l, j].reshape(DC, 128, 2, FC, 128).transpose(3, 2, 1, 0, 4).reshape(
                2 * FC, 128, DC * 128)
            packs[(f'dn{j}', l)] = wdn[l, j].reshape(2, FC // 2, 128, DC, 128).transpose(3, 0, 2, 1, 4).reshape(
                2 * DC, 128, FC * 64)
        if l % 2 == 0:
            e = l // 2
            W = f32(inp['w_in_ab'])[e]
            cl = []
            for base, nh in ((0, 8), (1024, 2), (1536, 8), (2560, 2)):
                cl += [base + hh * 128 + a128 for hh in range(nh)]
                cl += [base + hh * 128 + (a128 ^ 32) for hh in range(nh)]
            packs[('ab', l)] = _units(W, cl)
            vcols = np.concatenate([1280 + np.arange(256), 2816 + np.arange(256)])
            packs[('v', l)] = np.ascontiguousarray(W[:, vcols]).reshape(DC, 128, 512)
            Wo = f32(inp['w_out_ab'])[e]
        else:
            o = l // 2
            W = f32(inp['w_in_c'])[o]
            cl = [c * 128 + a128 for c in range(4)] + [512 + c * 128 + a128 for c in range(2)]
            cl += [np.concatenate([768 + a64, 768 + (a64 ^ 16)]), None]
            packs[('cin', l)] = _units(W, cl)
            Wq = f32(inp['w_uq'])[o]
            cl = []
            for hh in range(16):
                cl.append(hh * 192 + a128)
                cl.append(np.concatenate([hh * 192 + 128 + a64, hh * 192 + 128 + (a64 ^ 16)]))
            packs[('uq', l)] = _units(Wq, cl)
            Wk = f32(inp['w_ukv'])[o]
            packs[('ukk', l)] = _units(Wk, [hh * 256 + a128 for hh in range(16)])
            packs[('ukv', l)] = _units(Wk, [hh * 256 + 128 + a128 for hh in range(16)])
            Wo = f32(inp['w_out_c'])[o]
        packs[('wo', l)] = Wo.reshape(16, 128, DC, 128).transpose(2, 1, 0, 3).reshape(DC, 128, 16 * 128)
    for core in range(8):
        b, c4 = divmod(core, 4)
        m = maps[core]
        xt = np.concatenate([ctx[b, c4 * 64:(c4 + 1) * 64], x[b, c4 * TOK:(c4 + 1) * TOK]], axis=0)
        m['xT'] = np.ascontiguousarray(xt.T)
        m['wm'] = np.ascontiguousarray(wm_units[:, c4 * PC:(c4 + 1) * PC]).reshape(DEPTH * PC * 128, DC * 128)
        bsh = bm_units[:, c4 * PC:(c4 + 1) * PC]
        m['bm'] = np.ascontiguousarray(np.repeat(bsh.transpose(2, 0, 1)[:, :, :, None], 3, axis=3)).reshape(128, -1)
        m['cv'] = cvt
        s = np.zeros((128, 2), np.float32)
        s[:, b] = 1.0
        m['sel'] = s
        m['gn'], m['gf'], m['gqk'], m['snk'], m['gc'] = gnt, gft, gqk, snk, gct
        m['cs128'] = _rope_tables(cfg, c4, 128)
        m['cs64'] = _rope_tables(cfg, c4, 64)
        m['msk'] = _masks(cfg, c4)
        for (name, l), pk in packs.items():
            U = pk.shape[0]
            m[f'w_{name}_{l}'] = np.ascontiguousarray(pk[c4 * U // 4:(c4 + 1) * U // 4]).reshape(-1, pk.shape[2])
    return maps


def run(cfg, inputs):
    maps = prep(cfg, inputs)
    nc, S = build(cfg)
    res = run_bass_kernel_spmd(nc, maps, core_ids=list(range(8)))
    out = np.zeros((2, cfg.SEQ, cfg.D), np.float32)
    for core in range(8):
        b, c4 = divmod(core, 4)
        out[b, c4 * cfg.TOK:(c4 + 1) * cfg.TOK, :] = np.asarray(res.results[core]['yT'], dtype=np.float32).T
    return out


def kernel(**inputs):
    return run(Cfg(), inputs)
```
